# Optimizing a Trainium2 kernel written in Bass

```python
import math
import jax, jax.numpy as jnp
from jax import lax
import numpy as np

D_MODEL = 2048
BATCH = 1
SEQ = 8192
DEPTH = 4

CHUNK = 64
N_MEM = 256
LN_EPS = 1e-5
RMS_EPS = 1e-6
DEEPNORM_ALPHA = (2.0 * DEPTH) ** 0.25
DEEPNORM_BETA = (8.0 * DEPTH) ** -0.25

D_MIX = D_MODEL
GLA_HEADS = 4
GLA_DV = D_MIX // 2
GLA_HEAD_V = GLA_DV // GLA_HEADS
GLA_DK = GLA_DV // 2
GLA_HEAD_K = GLA_DK // GLA_HEADS
GLA_RANK = 16
GLA_TAU = 16.0
SSD_D_INNER = D_MIX - GLA_DV
SSD_HEADDIM = 64
SSD_HEADS = SSD_D_INNER // SSD_HEADDIM
SSD_STATE = 128
SSD_GROUPS = 2
SSD_CONV = 4
SSD_CONV_DIM = SSD_D_INNER + 2 * SSD_GROUPS * SSD_STATE
DT_MIN = 0.001
DT_MAX = 0.1
MIX_SPLITS = (GLA_DK, GLA_DK, GLA_DV, GLA_DV, GLA_RANK, SSD_D_INNER, SSD_CONV_DIM, SSD_HEADS)
D_IN = GLA_DK * 2 + GLA_DV * 2 + GLA_RANK + SSD_D_INNER + SSD_CONV_DIM + SSD_HEADS
XA_HEADS = 4
XA_HEAD_DIM = D_MODEL // XA_HEADS
D_FF = ((8 * D_MODEL // 3 + 255) // 256) * 256
FFN_CONV = 3

kernel_name = 'hybrid_gla_ssd_memxattn_deepnorm_trunk'


def _layer_norm(x, g, b):
    xf = x.astype(jnp.float32)
    mu = jnp.mean(xf, axis=-1, keepdims=True)
    var = jnp.mean(jnp.square(xf - mu), axis=-1, keepdims=True)
    return ((xf - mu) * lax.rsqrt(var + LN_EPS) * g + b).astype(x.dtype)


def _rms_norm(x, g):
    xf = x.astype(jnp.float32)
    return xf * lax.rsqrt(jnp.mean(jnp.square(xf), axis=-1, keepdims=True) + RMS_EPS) * g


def _causal_dwconv(x, w, b):
    k = w.shape[0]
    y = lax.conv_general_dilated(x, w[:, None, :], window_strides=(1,), padding=[(k - 1, 0)],
                                 dimension_numbers=('NWC', 'WIO', 'NWC'),
                                 feature_group_count=x.shape[-1])
    return y + b


def _chunk_states(decay, contrib):
    dec = jnp.moveaxis(decay, 1, 0)
    con = jnp.moveaxis(contrib, 1, 0)

    def step(state, inp):
        d, c = inp
        return d * state + c, state

    _, states = lax.scan(step, jnp.zeros_like(con[0]), (dec, con))
    return jnp.moveaxis(states, 0, 1)


def _gla(q, k, v, log_g):
    bsz, s, h, dk = q.shape
    dv = v.shape[-1]
    nc = s // CHUNK
    q = q.reshape(bsz, nc, CHUNK, h, dk) * (dk ** -0.5)
    k = k.reshape(bsz, nc, CHUNK, h, dk)
    v = v.reshape(bsz, nc, CHUNK, h, dv)
    b = jnp.cumsum(log_g.reshape(bsz, nc, CHUNK, h, dk), axis=2)
    b_last = b[:, :, -1]
    q_dec = q * jnp.exp(b)
    k_inv = k * jnp.exp(-b)
    k_end = k * jnp.exp(b_last[:, :, None] - b)
    causal = jnp.tril(jnp.ones((CHUNK, CHUNK), dtype=bool))
    scores = jnp.where(causal, jnp.einsum('bclhd,bcmhd->bchlm', q_dec, k_inv), 0.0)
    o_intra = jnp.einsum('bchlm,bcmhv->bclhv', scores, v)
    contrib = jnp.einsum('bclhd,bclhv->bchdv', k_end, v)
    states = _chunk_states(jnp.exp(b_last)[..., None], contrib)
    o_inter = jnp.einsum('bclhd,bchdv->bclhv', q_dec, states)
    return (o_intra + o_inter).reshape(bsz, s, h, dv)


def _ssd(x, dt, a, bm, cm):
    bsz, s, h, p = x.shape
    g, n = bm.shape[-2:]
    hpg = h // g
    nc = s // CHUNK
    xdt = (x * dt[..., None]).reshape(bsz, nc, CHUNK, g, hpg, p)
    cs = jnp.cumsum((dt * a).reshape(bsz, nc, CHUNK, g, hpg), axis=2)
    bc = bm.reshape(bsz, nc, CHUNK, g, n)
    cc = cm.reshape(bsz, nc, CHUNK, g, n)
    causal = jnp.tril(jnp.ones((CHUNK, CHUNK), dtype=bool))
    seg = cs[:, :, :, None] - cs[:, :, None]
    lmat = jnp.exp(jnp.where(causal[:, :, None, None], seg, -jnp.inf))
    cb = jnp.einsum('bclgn,bcmgn->bclmg', cc, bc)
    y_diag = jnp.einsum('bclmg,bclmgh,bcmghp->bclghp', cb, lmat, xdt)
    cs_last = cs[:, :, -1]
    contrib = jnp.einsum('bclgn,bclgh,bclghp->bcghpn', bc, jnp.exp(cs_last[:, :, None] - cs), xdt)
    states = _chunk_states(jnp.exp(cs_last)[..., None, None], contrib)
    y_off = jnp.einsum('bclgn,bcghpn,bclgh->bclghp', cc, states, jnp.exp(cs))
    return (y_diag + y_off).reshape(bsz, s, h, p)


def _hybrid_mixer(x, w_in, gla_w_gate, gla_b_gate, gla_norm_w, ssd_conv_w, ssd_conv_b,
                  ssd_dt_bias, ssd_a_log, ssd_d, ssd_norm_w, w_out):
    f32 = jnp.float32
    bsz, s, _ = x.shape
    split_idx = tuple(int(i) for i in np.cumsum(MIX_SPLITS)[:-1])
    q, k, v, og, a_lr, z, xbc, dt = jnp.split(x @ w_in, split_idx, axis=-1)
    log_g = jax.nn.log_sigmoid((a_lr @ gla_w_gate + gla_b_gate).astype(f32)) / GLA_TAU
    o_gla = _gla(q.astype(f32).reshape(bsz, s, GLA_HEADS, GLA_HEAD_K),
                 k.astype(f32).reshape(bsz, s, GLA_HEADS, GLA_HEAD_K),
                 v.astype(f32).reshape(bsz, s, GLA_HEADS, GLA_HEAD_V),
                 log_g.reshape(bsz, s, GLA_HEADS, GLA_HEAD_K))
    o_gla = _rms_norm(o_gla, gla_norm_w.reshape(GLA_HEADS, GLA_HEAD_V)).reshape(bsz, s, GLA_DV)
    o_gla = o_gla * jax.nn.silu(og.astype(f32))
    xbc = jax.nn.silu(_causal_dwconv(xbc, ssd_conv_w, ssd_conv_b)).astype(f32)
    xs, bm, cm = jnp.split(xbc, [SSD_D_INNER, SSD_D_INNER + SSD_GROUPS * SSD_STATE], axis=-1)
    dt = jax.nn.softplus(dt.astype(f32) + ssd_dt_bias)
    a = -jnp.exp(ssd_a_log.astype(f32))
    xs = xs.reshape(bsz, s, SSD_HEADS, SSD_HEADDIM)
    y = _ssd(xs, dt, a, bm.reshape(bsz, s, SSD_GROUPS, SSD_STATE),
             cm.reshape(bsz, s, SSD_GROUPS, SSD_STATE))
    y = (y + ssd_d[:, None] * xs).reshape(bsz, s, SSD_D_INNER) * jax.nn.silu(z.astype(f32))
    y = _rms_norm(y.reshape(bsz, s, SSD_GROUPS, SSD_D_INNER // SSD_GROUPS),
                  ssd_norm_w.reshape(SSD_GROUPS, SSD_D_INNER // SSD_GROUPS)).reshape(bsz, s, SSD_D_INNER)
    mixed = jnp.concatenate([o_gla, y], axis=-1).astype(x.dtype)
    return mixed @ w_out


def _cross_attention(x, mem, wq, wk, wv, wo):
    bsz, s, d = x.shape
    m = mem.shape[1]
    q = (x @ wq).reshape(bsz, s, XA_HEADS, XA_HEAD_DIM)
    k = (mem @ wk).reshape(bsz, m, XA_HEADS, XA_HEAD_DIM)
    v = (mem @ wv).reshape(bsz, m, XA_HEADS, XA_HEAD_DIM)
    scores = jnp.einsum('bshd,bmhd->bhsm', q, k).astype(jnp.float32) * (XA_HEAD_DIM ** -0.5)
    p = jax.nn.softmax(scores, axis=-1).astype(x.dtype)
    o = jnp.einsum('bhsm,bmhd->bshd', p, v).reshape(bsz, s, d)
    return o @ wo


def _conv_ffn(x, w_up, conv_w, conv_b, w_down):
    gate, val = jnp.split(x @ w_up, 2, axis=-1)
    gate = _causal_dwconv(gate, conv_w, conv_b)
    return (jax.nn.gelu(gate, approximate=False) * val) @ w_down


def setup_inputs(seed: int = 0) -> dict:
    key = jax.random.key(seed)
    ks = jax.random.split(key, 32)
    f32 = jnp.float32
    L = DEPTH

    def nrm(k, shape, scale):
        return jax.random.normal(k, shape, f32) * scale

    segs = ((GLA_DK, 1.0), (GLA_DK, 1.0), (GLA_DV, DEEPNORM_BETA), (GLA_DV, 1.0), (GLA_RANK, 1.0),
            (SSD_D_INNER, 1.0), (SSD_D_INNER, DEEPNORM_BETA), (2 * SSD_GROUPS * SSD_STATE, 1.0),
            (SSD_HEADS, 1.0))
    col_scale = jnp.concatenate([jnp.full((n,), sc, f32) for n, sc in segs])
    dt0 = jnp.exp(jax.random.uniform(ks[8], (L, SSD_HEADS), f32, math.log(DT_MIN), math.log(DT_MAX)))
    return {
        'x': nrm(ks[0], (BATCH, SEQ, D_MODEL), 1.0),
        'mem': nrm(ks[1], (BATCH, N_MEM, D_MODEL), 1.0),
        'w_in': nrm(ks[2], (L, D_MODEL, D_IN), D_MODEL ** -0.5) * col_scale,
        'gla_w_gate': nrm(ks[3], (L, GLA_RANK, GLA_DK), GLA_RANK ** -0.5),
        'gla_b_gate': nrm(ks[4], (L, GLA_DK), 0.02),
        'gla_norm_w': 1.0 + nrm(ks[5], (L, GLA_DV), 0.02),
        'ssd_conv_w': nrm(ks[6], (L, SSD_CONV, SSD_CONV_DIM), SSD_CONV ** -0.5),
        'ssd_conv_b': nrm(ks[7], (L, SSD_CONV_DIM), 0.02),
        'ssd_dt_bias': dt0 + jnp.log(-jnp.expm1(-dt0)),
        'ssd_a_log': jnp.log(jax.random.uniform(ks[9], (L, SSD_HEADS), f32, 1.0, 16.0)),
        'ssd_d': 1.0 + nrm(ks[10], (L, SSD_HEADS), 0.02),
        'ssd_norm_w': 1.0 + nrm(ks[11], (L, SSD_D_INNER), 0.02),
        'w_out': nrm(ks[12], (L, D_MIX, D_MODEL), D_MIX ** -0.5 * DEEPNORM_BETA),
        'ln_mix_g': 1.0 + nrm(ks[13], (L, D_MODEL), 0.02),
        'ln_mix_b': nrm(ks[14], (L, D_MODEL), 0.02),
        'xa_wq': nrm(ks[15], (L, D_MODEL, D_MODEL), D_MODEL ** -0.5),
        'xa_wk': nrm(ks[16], (L, D_MODEL, D_MODEL), D_MODEL ** -0.5),
        'xa_wv': nrm(ks[17], (L, D_MODEL, D_MODEL), D_MODEL ** -0.5 * DEEPNORM_BETA),
        'xa_wo': nrm(ks[18], (L, D_MODEL, D_MODEL), D_MODEL ** -0.5 * DEEPNORM_BETA),
        'ln_xa_g': 1.0 + nrm(ks[19], (L, D_MODEL), 0.02),
        'ln_xa_b': nrm(ks[20], (L, D_MODEL), 0.02),
        'ffn_w_up': nrm(ks[21], (L, D_MODEL, 2 * D_FF), D_MODEL ** -0.5 * DEEPNORM_BETA),
        'ffn_conv_w': nrm(ks[22], (L, FFN_CONV, D_FF), FFN_CONV ** -0.5),
        'ffn_conv_b': nrm(ks[23], (L, D_FF), 0.02),
        'ffn_w_down': nrm(ks[24], (L, D_FF, D_MODEL), D_FF ** -0.5 * DEEPNORM_BETA),
        'ln_ffn_g': 1.0 + nrm(ks[25], (L, D_MODEL), 0.02),
        'ln_ffn_b': nrm(ks[26], (L, D_MODEL), 0.02),
    }


def reference(x, mem, w_in, gla_w_gate, gla_b_gate, gla_norm_w, ssd_conv_w, ssd_conv_b,
              ssd_dt_bias, ssd_a_log, ssd_d, ssd_norm_w, w_out, ln_mix_g, ln_mix_b,
              xa_wq, xa_wk, xa_wv, xa_wo, ln_xa_g, ln_xa_b,
              ffn_w_up, ffn_conv_w, ffn_conv_b, ffn_w_down, ln_ffn_g, ln_ffn_b):
    for l in range(DEPTH):
        h = _hybrid_mixer(x, w_in[l], gla_w_gate[l], gla_b_gate[l], gla_norm_w[l],
                          ssd_conv_w[l], ssd_conv_b[l], ssd_dt_bias[l], ssd_a_log[l],
                          ssd_d[l], ssd_norm_w[l], w_out[l])
        x = _layer_norm(DEEPNORM_ALPHA * x + h, ln_mix_g[l], ln_mix_b[l])
        h = _cross_attention(x, mem, xa_wq[l], xa_wk[l], xa_wv[l], xa_wo[l])
        x = _layer_norm(DEEPNORM_ALPHA * x + h, ln_xa_g[l], ln_xa_b[l])
        h = _conv_ffn(x, ffn_w_up[l], ffn_conv_w[l], ffn_conv_b[l], ffn_w_down[l])
        x = _layer_norm(DEEPNORM_ALPHA * x + h, ln_ffn_g[l], ln_ffn_b[l])
    return x
```

```python
import contextlib
import numpy as np
import concourse.bass as bass
import concourse.mybir as mybir
from concourse.bass_utils import run_bass_kernel_spmd

F32 = mybir.dt.float32
BF16 = mybir.dt.bfloat16
AF = mybir.ActivationFunctionType
ALU = mybir.AluOpType
AX = mybir.AxisListType

NCORES = 8
DEPTH = 4
D = 2048
T = 1024
NT = T // 128
SEQ = 8192
D_IN = 5664
D_FF = 5632
NMEM = 256
ALPHA = (2.0 * DEPTH) ** 0.25
LN_EPS = 1e-5
RMS_EPS = 1e-6
ENGS = ("pe", "act", "dve", "pool", "sp")


def _prod(xs):
    r = 1
    for x in xs:
        r *= int(x)
    return r


class Prog:
    def __init__(self, nc, stack):
        self.nc = nc
        self.stack = stack
        self.q = {e: [] for e in ENGS}
        self.cnt = {e: 0 for e in ENGS}
        self.sems = {}
        for e in ENGS:
            self.sems[e] = stack.enter_context(nc.semaphore("s_" + e))
        self.seen = {e: {} for e in ENGS}
        self.recs = {}
        self.dmacnt = {}
        self.nops = 0
        self.dumps = []

    def region(self, ap):
        t = ap.tensor
        name = t.name
        a = ap.ap
        off = int(ap.offset)
        row = _prod(t.shape[1:]) if len(t.shape) > 1 else 1
        p0 = off // row
        f0 = off % row
        pe = 0
        fe = 0
        for s_, c_ in a:
            s_, c_ = abs(int(s_)), int(c_)
            pe += (c_ - 1) * (s_ // row)
            fe += (c_ - 1) * (s_ % row)
        if f0 + fe >= row:
            return name, p0, p0 + pe + (f0 + fe) // row, 0, row - 1
        return name, p0, p0 + pe, f0, f0 + fe

    def _sem(self, key, dma=True):
        if key not in self.sems:
            self.sems[key] = self.stack.enter_context(self.nc.semaphore("d_" + str(key)[:40]))
            if dma:
                self.dmacnt[key] = 0
        return self.sems[key]

    def op(self, eng, fn, reads=(), writes=(), dma_key=None, own_sem=None):
        waits = {}
        regs_r = [self.region(a) for a in reads]
        regs_w = [self.region(a) for a in writes]

        def conflicts(reg, is_write):
            name, p0, p1, f0, f1 = reg
            for r in self.recs.get(name, ()):
                (q0, q1, g0, g1, key, val, w, reng) = r
                if not (is_write or w):
                    continue
                if q1 < p0 or p1 < q0 or g1 < f0 or f1 < g0:
                    continue
                if reng == "pe" and eng == "pe":
                    continue
                if key in self.dmacnt:
                    val = 16 * self.dmacnt[key]
                if waits.get(key, 0) < val:
                    waits[key] = val

        for rg in regs_r:
            conflicts(rg, False)
        for rg in regs_w:
            conflicts(rg, True)
        wl = []
        for key, val in waits.items():
            if self.seen[eng].get(key, 0) >= val:
                continue
            self.seen[eng][key] = val
            wl.append((self.sems[key], val))
        if dma_key is not None:
            sem = self._sem(dma_key)
            self.dmacnt[dma_key] += 1
            key, val, inc = dma_key, 16 * self.dmacnt[dma_key], 16
        elif own_sem is not None:
            key = own_sem
            sem = self._sem(key, dma=False)
            val, inc = 1, 1
        else:
            self.cnt[eng] += 1
            key, val, inc, sem = eng, self.cnt[eng], 1, self.sems[eng]
        self.q[eng].append((wl, fn, sem, inc))
        self.nops += 1
        for (name, p0, p1, f0, f1) in regs_w:
            lst = self.recs.setdefault(name, [])
            lst[:] = [r for r in lst if not (p0 <= r[0] and r[1] <= p1 and f0 <= r[2] and r[3] <= f1)]
            lst.append((p0, p1, f0, f1, key, val, True, eng))
        for (name, p0, p1, f0, f1) in regs_r:
            lst = self.recs.setdefault(name, [])
            lst[:] = [r for r in lst if not ((not r[6]) and r[4] == key and p0 <= r[0] and r[1] <= p1
                                             and f0 <= r[2] and r[3] <= f1)]
            lst.append((p0, p1, f0, f1, key, val, False, eng))

    def barrier(self):
        targets = []
        for e in ENGS:
            if self.cnt[e] > 0:
                targets.append((e, self.cnt[e]))
        for k, c in self.dmacnt.items():
            if c > 0:
                targets.append((k, 16 * c))
        for k in self.sems:
            if k not in ENGS and k not in self.dmacnt:
                targets.append((k, 1))
        for e in ENGS:
            wl = []
            for key, val in targets:
                if self.seen[e].get(key, 0) >= val:
                    continue
                self.seen[e][key] = val
                wl.append((self.sems[key], val))
            if wl:
                self.q[e].append((wl, None, None, 0))
        self.recs = {}

    def emit(self):
        nc = self.nc
        qs = self.q
        with nc.Block() as block:
            def run(engine, lst):
                for (wl, fn, sem, inc) in lst:
                    for (s, v) in wl:
                        engine.wait_ge(s, v)
                    if fn is not None:
                        fn(engine).then_inc(sem, inc)

            @block.tensor
            def _(e):
                run(e, qs["pe"])

            @block.scalar
            def _(e):
                run(e, qs["act"])

            @block.vector
            def _(e):
                run(e, qs["dve"])

            @block.gpsimd
            def _(e):
                run(e, qs["pool"])

            @block.sync
            def _(e):
                run(e, qs["sp"])

    def dma(self, out, in_, eng="sp", key=None):
        if key is None:
            if not type(out.tensor).__name__.startswith("DRam"):
                side = out
            elif not type(in_.tensor).__name__.startswith("DRam"):
                side = in_
            else:
                side = out
            name, p0, _, f0, _ = self.region(side)
            if not name.startswith("slot"):
                name = "".join(ch for ch in name if not ch.isdigit())
            key = f"{name}@{p0}_{f0}"
        self.op(eng, lambda e: e.dma_start(out=out, in_=in_), [in_], [out], dma_key=key)

    def mm(self, out, lhsT, rhs, start=True, stop=True):
        self.op("pe", lambda e: e.matmul(out, lhsT, rhs, start=start, stop=stop), [lhsT, rhs], [out])

    def tr(self, out, in_, ident):
        self.op("pe", lambda e: e.transpose(out, in_, ident), [in_, ident], [out])

    def act(self, out, in_, func, bias=None, scale=1.0, accum_out=None):
        reads = [in_]
        kw = {}
        if bias is not None:
            kw["bias"] = bias
            if not isinstance(bias, (int, float)):
                reads.append(bias)
        if not isinstance(scale, (int, float)):
            reads.append(scale)
        kw["scale"] = scale
        writes = [out]
        if accum_out is not None:
            kw["accum_out"] = accum_out
            writes.append(accum_out)
        self.op("act", lambda e: e.activation(out=out, in_=in_, func=func, **kw), reads, writes)

    def tt(self, out, in0, in1, op, eng="dve"):
        self.op(eng, lambda e: e.tensor_tensor(out=out, in0=in0, in1=in1, op=op), [in0, in1], [out])

    def ts(self, out, in0, s1, op0, s2=None, op1=None, eng="dve"):
        reads = [in0] + [s for s in (s1, s2) if s is not None and not isinstance(s, (int, float))]
        kw = {}
        if op1 is not None:
            kw["op1"] = op1
        self.op(eng, lambda e: e.tensor_scalar(out=out, in0=in0, scalar1=s1, scalar2=s2, op0=op0, **kw), reads, [out])

    def stt(self, out, in0, scalar, in1, op0, op1):
        reads = [in0, in1] + ([scalar] if not isinstance(scalar, (int, float)) else [])
        self.op("dve", lambda e: e.scalar_tensor_tensor(out=out, in0=in0, scalar=scalar, in1=in1, op0=op0, op1=op1),
                reads, [out])

    def copy(self, out, in_, eng="dve"):
        if eng == "act":
            self.op("act", lambda e: e.copy(out=out, in_=in_), [in_], [out])
        else:
            self.op(eng, lambda e: e.tensor_copy(out=out, in_=in_), [in_], [out])

    def memset(self, ap, val, eng="dve"):
        self.op(eng, lambda e: e.memset(ap, val), [], [ap])

    def scan(self, out, d0, d1, initial, op0, op1):
        reads = [d0, d1] + ([initial] if not isinstance(initial, (int, float)) else [])
        self.op("dve", lambda e: e.tensor_tensor_scan(out=out, data0=d0, data1=d1, initial=initial, op0=op0, op1=op1),
                reads, [out])

    def reduce(self, out, in_, op, axis=AX.X):
        self.op("dve", lambda e: e.tensor_reduce(out=out, in_=in_, axis=axis, op=op), [in_], [out])

    def bn_stats(self, out, in_):
        self.op("dve", lambda e: e.bn_stats(out=out, in_=in_), [in_], [out])

    def bn_aggr(self, out, in_):
        self.op("dve", lambda e: e.bn_aggr(out=out, in_=in_), [in_], [out])

    def collective(self, src, dst, name):
        self.op("pool", lambda e: e.collective_compute("AllGather", ALU.bypass,
                                                       replica_groups=[list(range(NCORES))],
                                                       ins=[src], outs=[dst]), [src], [dst], own_sem="cc_" + name)
        key = "cc_" + name
        self.seen["pool"][key] = 1
        self.q["pool"].append(([(self.sems[key], 1)], None, None, 0))

    def dump(self, name, ap, eng="sp"):
        d = self.nc.dram_tensor("dbg_" + name, list(ap.shape), ap.dtype, kind="ExternalOutput").ap()
        self.dma(d, ap, eng=eng)
        self.dumps.append("dbg_" + name)


U_GROUPS = (6, 6, 5, 5)


def weight_blocks():
    B = []
    for h in range(4):
        B.append((f"G1_{h}", "w_in", 0, D, [(512 + 128 * h, 128), (1024 + 256 * h, 256), (128 * h, 128)]))
        B.append((f"G2_{h}", "w_in", 0, D, [(2048 + 256 * h, 256)]))
    for g in range(2):
        B.append((f"S1_{g}", "w_in", 0, D, [(4112 + 512 * g, 512)]))
        B.append((f"S2_{g}", "w_in", 0, D, [(5136 + 128 * g, 128), (5392 + 128 * g, 128)]))
        B.append((f"S3_{g}", "w_in", 0, D, [(3088 + 512 * g, 512)]))
    for j in range(4):
        B.append((f"O_{j}", "w_out", 0, D, [(512 * j, 512)]))
    for j in range(4):
        B.append((f"K_{j}", "xa_wk", 0, D, [(512 * j, 512)]))
    for j in range(4):
        B.append((f"V_{j}", "xa_wv", 0, D, [(512 * j, 512)]))
    for j in range(4):
        B.append((f"Q_{j}", "xa_wq", 0, D, [(512 * j, 512)]))
    for j in range(4):
        B.append((f"WO_{j}", "xa_wo", 0, D, [(512 * j, 512)]))
    for i in range(22):
        B.append((f"U_{i}", "ffn_w_up", 0, D, [(256 * i, 256), (D_FF + 256 * i, 256)]))
    c0 = 0
    for gi, nu in enumerate(U_GROUPS):
        nk = 2 * nu
        for j in range(4):
            B.append((f"D_{gi}_{j}", "ffn_w_down", c0 * 128, nk * 128, [(512 * j, 512)]))
        c0 += nk
    return B


def block_table():
    tab = {}
    off = 0
    for (name, mat, r0, nr, cols) in weight_blocks():
        nk = nr // 128
        ncols = sum(c for _, c in cols)
        tab[name] = (off, nk, ncols)
        off += nk * ncols
    return tab, off


BTAB, WL = block_table()
PIECE = 262144
assert WL == 2 * PIECE
for _n, (_o, _k, _c) in BTAB.items():
    assert _o // PIECE == (_o + _k * _c - 1) // PIECE, _n

PC_NEGB, PC_CW, PC_CB, PC_FW, PC_FB = 0, 4, 52, 64, 196
PC_N = 240
PR_GNW, PR_SNW, PR_DTB, PR_ALOG, PR_SD = 0, 1024, 2048, 2064, 2080
PR_LN = 2096
PR_N = PR_LN + 6 * D
C_IDENT, C_CAUSAL, C_STRICT, C_ONES, C_RESET = 0, 128, 256, 384, 512
C_MISC = 512 + T
CN = C_MISC + 8


def host_weights(inp, l):
    out = np.zeros((128, WL), np.float32)
    for (name, mat, r0, nr, cols) in weight_blocks():
        off, nk, ncols = BTAB[name]
        W = inp[mat][l]
        sub = np.concatenate([W[r0:r0 + nr, c0:c0 + c] for c0, c in cols], axis=1)
        out[:, off:off + nk * ncols] = sub.reshape(nk, 128, ncols).transpose(1, 0, 2).reshape(128, nk * ncols)
    return out


def host_sblock(inp, l):
    W = inp["w_in"][l]
    sub = np.concatenate([W[:, 3072:3088], W[:, 5648:5664]], axis=1)
    return np.ascontiguousarray(sub.reshape(16, 128, 32).transpose(1, 0, 2).reshape(128, 512))


def host_small(inp, l):
    pc = np.zeros((128, PC_N), np.float32)
    pc[:, PC_NEGB:PC_NEGB + 4] = -inp["gla_b_gate"][l].reshape(4, 128).T
    cw = inp["ssd_conv_w"][l]
    pc[:, PC_CW:PC_CW + 48] = cw.reshape(4, 12, 128).transpose(2, 1, 0).reshape(128, 48)
    pc[:, PC_CB:PC_CB + 12] = inp["ssd_conv_b"][l].reshape(12, 128).T
    fw = inp["ffn_conv_w"][l]
    pc[:, PC_FW:PC_FW + 132] = fw.reshape(3, 44, 128).transpose(2, 1, 0).reshape(128, 132)
    pc[:, PC_FB:PC_FB + 44] = inp["ffn_conv_b"][l].reshape(44, 128).T
    pr = np.zeros((1, PR_N), np.float32)
    pr[0, PR_GNW:PR_GNW + 1024] = inp["gla_norm_w"][l]
    pr[0, PR_SNW:PR_SNW + 1024] = inp["ssd_norm_w"][l]
    pr[0, PR_DTB:PR_DTB + 16] = inp["ssd_dt_bias"][l]
    pr[0, PR_ALOG:PR_ALOG + 16] = inp["ssd_a_log"][l]
    pr[0, PR_SD:PR_SD + 16] = inp["ssd_d"][l]
    for i, k in enumerate(("ln_mix_g", "ln_mix_b", "ln_xa_g", "ln_xa_b", "ln_ffn_g", "ln_ffn_b")):
        pr[0, PR_LN + i * D:PR_LN + (i + 1) * D] = inp[k][l]
    return pc, pr, np.ascontiguousarray(inp["gla_w_gate"][l])


def host_consts():
    c = np.zeros((128, CN), np.float32)
    i = np.arange(128)
    c[:, C_IDENT:C_IDENT + 128] = np.eye(128)
    c[:, C_CAUSAL:C_CAUSAL + 128] = (i[:, None] <= i[None, :])
    c[:, C_STRICT:C_STRICT + 128] = (i[:, None] > i[None, :])
    c[:, C_ONES:C_ONES + 128] = 1.0
    r = np.ones(T, np.float32)
    r[::128] = 0.0
    c[:, C_RESET:C_RESET + T] = r[None, :]
    c[:, C_MISC + 0] = LN_EPS
    c[:, C_MISC + 1] = RMS_EPS
    c[:, C_MISC + 2] = 1.0
    c[:, C_MISC + 3] = np.log(128.0 ** -0.5)
    c[:, C_MISC + 4] = 0.0
    return c


class Builder:
    def __init__(self, n_layers, stop_after=None, cc=True, wl_dev=None, kind="F", wl_base=0):
        self.nl = n_layers
        self.stop_after = stop_after
        self.cc = cc
        self.kind = kind
        self.wl_base = wl_base
        self.wl_dev = WL if wl_dev is None else wl_dev
        nc = self.nc = bass.Bass("TRN2", target_bir_lowering=False)
        self.root = contextlib.ExitStack()
        P = self.P = Prog(nc, self.root)
        dt = nc.dram_tensor
        self.x_in = dt("x", [T, D], F32, kind="ExternalInput").ap()
        self.xh_in = dt("xhalo", [128, 16, 3], F32, kind="ExternalInput").ap()
        self.memT_in = dt("memT", [128, 16, NMEM], F32, kind="ExternalInput").ap()
        self.consts_in = dt("consts", [128, CN], F32, kind="ExternalInput").ap()
        self.cmask_in = dt("cmask", [128, 24], F32, kind="ExternalInput").ap()
        if cc:
            self.wsh = [dt(f"wsh{l}", [16, WL], F32, kind="ExternalInput").ap() for l in range(n_layers)]
        else:
            self.wsh = [dt(f"wsh{l}", [128, self.wl_dev], F32, kind="ExternalInput").ap() for l in range(n_layers)]
        self.ws_in = dt("wsblk", [n_layers, 128, 512], F32, kind="ExternalInput").ap()
        self.pcol_in = dt("pcol", [n_layers, 128, PC_N], F32, kind="ExternalInput").ap()
        self.prow_in = dt("prow", [n_layers, PR_N], F32, kind="ExternalInput").ap()
        self.wgate_in = dt("wgate", [n_layers, 16, 512], F32, kind="ExternalInput").ap()
        if kind == "A":
            self.outC = dt("outC", [128, 2048], F32, kind="ExternalOutput").ap()
            self.outD = dt("outD", [128, 2048], F32, kind="ExternalOutput").ap()
        else:
            self.out = dt("out", [T, D], F32, kind="ExternalOutput").ap()
        if kind == "B":
            self.gC_in = dt("gC", [1024, 2048], F32, kind="ExternalInput").ap()
            self.gD_in = dt("gD", [1024, 2048], F32, kind="ExternalInput").ap()
        self.XD = dt("XD", [T, D], F32, kind="Internal").ap()
        if cc:
            self.wb = [[dt(f"wb{l}_{k}", [512, 8192], BF16, kind="Internal").ap() for k in range(2)] for l in range(n_layers)]
            self.wg_t = [[dt(f"wg{l}_{k}", [4096, 8192], BF16, kind="Internal").ap() for k in range(2)] for l in range(n_layers)]
            self.wg = [[w.rearrange("(q a) b -> q (a b)", a=32) for w in ws_] for ws_ in self.wg_t]
        else:
            self.wg1 = None
        self.ncc = 0

    def sb(self, stack, name, shape, dtype):
        return stack.enter_context(self.nc.sbuf_tensor(name, shape, dtype))

    def bank(self):
        b = self.ps[self.pi % 8]
        self.pi += 1
        return b

    def exch(self, name, src_sb):
        nc = self.nc
        s = nc.dram_tensor(f"ex_s_{name}", [128, 2048], F32, kind="Internal").ap()
        d = nc.dram_tensor(f"ex_d_{name}", [1024, 2048], F32, kind="Internal").ap()
        self.P.dma(s, src_sb, eng="pool")
        if self.cc:
            self.P.collective(s, d, name)
        else:
            for r in range(NCORES):
                self.P.dma(d[r * 128:(r + 1) * 128, :], s, eng="pool")
        return d

    def wload(self, l, name):
        off, nk, ncols = BTAB[name]
        slot = self.slots[self.si % len(self.slots)]
        self.si += 1
        n = nk * ncols
        if self.cc:
            k = off // PIECE
            src = self.wg[l][k][:, off - k * PIECE:off - k * PIECE + n]
        else:
            src = self.wsh[l][:, off - self.wl_base:off - self.wl_base + n]
            self.P.dma(slot[:, 0:n], src, eng="pool")
            return slot[:, 0:n].rearrange("p (k c) -> p k c", k=nk)
        self.P.dma(slot[:, 0:n], src, eng="sp")
        return slot[:, 0:n].rearrange("p (k c) -> p k c", k=nk)

    def prefetch(self, l, names):
        views = {}
        order = list(names)
        depth = len(self.slots) - 1
        state = {"next": 0}

        def get(i):
            while state["next"] < len(order) and state["next"] <= i + depth - 1:
                j = state["next"]
                views[j] = self.wload(l, order[j])
                state["next"] += 1
            return views.pop(i)
        return get

    def build(self):
        nc, P = self.nc, self.P
        root = self.root
        self.ps = [root.enter_context(nc.psum_tensor(f"ps{i}", [128, 512], F32)) for i in range(8)]
        self.pi = 0
        self.XT = self.sb(root, "XT", [128, 16, T], BF16)
        self.XTh = self.sb(root, "XTh", [128, 16, 4], BF16)
        self.slots = [self.sb(root, f"slot{i}", [128, 8192], BF16) for i in range(3)]
        self.si = 0
        cst = self.cst = self.sb(root, "cst", [128, CN], F32)
        self.cm = self.sb(root, "cmask_sb", [128, 24], F32)
        self.identB = self.sb(root, "identB", [128, 128], BF16)
        self.causalB = self.sb(root, "causalB", [128, 128], BF16)
        P.dma(cst[:], self.consts_in)
        P.dma(self.cm[:], self.cmask_in)
        self.identF = cst[:, C_IDENT:C_IDENT + 128]
        self.causal = cst[:, C_CAUSAL:C_CAUSAL + 128]
        self.strict = cst[:, C_STRICT:C_STRICT + 128]
        self.ones = cst[:, C_ONES:C_ONES + 128]
        self.reset = cst[:, C_RESET:C_RESET + T]
        self.c_lneps = cst[:, C_MISC + 0:C_MISC + 1]
        self.c_rmseps = cst[:, C_MISC + 1:C_MISC + 2]
        self.c_one = cst[:, C_MISC + 2:C_MISC + 3]
        self.c_lnq = cst[:, C_MISC + 3:C_MISC + 4]
        P.copy(self.identB[:], self.identF)
        P.copy(self.causalB[:], self.causal)
        self.gather_weights(0)
        with contextlib.ExitStack() as st:
            xs = self.sb(st, "xs0", [128, 2, D], F32)
            hs = self.sb(st, "hs0", [128, 16, 3], F32)
            P.dma(hs[:], self.xh_in)
            P.copy(self.XTh[:, :, 0:3], hs[:])
            for t in range(NT):
                P.dma(xs[:, t % 2, :], self.x_in[t * 128:(t + 1) * 128, :])
                self.make_xT(t, xs[:, t % 2, :])
            P.barrier()
        if self.kind == "F":
            for l in range(self.nl):
                self.layer(l)
                if self.stop_after is not None and l == self.nl - 1:
                    break
        else:
            with contextlib.ExitStack() as lst:
                self.pc = self.sb(lst, "pc0", [128, PC_N], F32)
                P.dma(self.pc[:], self.pcol_in[0])
                if self.kind == "A":
                    self.mixer_A()
                elif self.kind == "B":
                    self.mixer_B()
                    self.xattn(0, last=True)
                else:
                    self.ffn(0, last=True, standalone=True)
                P.barrier()
        P.barrier()
        P.emit()
        self.root.close()
        return nc

    def gather_weights(self, l):
        P = self.P
        if self.cc:
            for k in range(2):
                P.dma(self.wb[l][k].rearrange("(p a) b -> p a b", a=32),
                      self.wsh[l][:, k * PIECE:(k + 1) * PIECE].rearrange("p (a b) -> p a b", a=32), eng="pool")
                P.collective(self.wb[l][k], self.wg_t[l][k], f"w{l}_{k}")
        else:
            pass

    def make_xT(self, t, xtile):
        P = self.P
        for cg in range(4):
            b = self.bank()
            for j in range(4):
                c = 4 * cg + j
                P.tr(b[:, j * 128:(j + 1) * 128], xtile[:, c * 128:(c + 1) * 128], self.identF)
            dst = self.XT[:, 4 * cg:4 * cg + 4, t * 128:(t + 1) * 128]
            src = b[:].rearrange("p (j c) -> p j c", j=4)
            if cg % 2 == 0:
                P.copy(dst, src, eng="act")
            else:
                P.copy(dst, src, eng="dve")

    def layer_norm_tile(self, st, t, xr_t, gb, bb, l, last):
        P = self.P
        stats = self.ln_stats
        for j in range(4):
            P.bn_stats(stats[:, j, :], xr_t[:, j * 512:(j + 1) * 512])
        P.bn_aggr(self.ln_mv[:, 0:2], stats[:].rearrange("p a b -> p (a b)"))
        P.act(self.ln_mv[:, 2:3], self.ln_mv[:, 1:2], AF.Ln, bias=self.c_lneps, scale=1.0)
        P.act(self.ln_mv[:, 3:4], self.ln_mv[:, 2:3], AF.Exp, scale=-0.5)
        P.ts(xr_t, xr_t, self.ln_mv[:, 0:1], ALU.subtract, self.ln_mv[:, 3:4], ALU.mult)
        P.tt(xr_t, xr_t, gb, ALU.mult, eng="pool")
        P.tt(xr_t, xr_t, bb, ALU.add, eng="pool")
        dst = self.out if last else self.XD
        P.dma(dst[t * 128:(t + 1) * 128, :], xr_t)
        self.make_xT(t, xr_t)

    def load_ln_params(self, st, l, idx):
        P = self.P
        gb = self.sb(st, f"lng{l}_{idx}", [128, D], F32)
        bb = self.sb(st, f"lnb{l}_{idx}", [128, D], F32)
        o = PR_LN + 2 * idx * D
        P.dma(gb[:], self.prow_in[l:l + 1, o:o + D].to_broadcast([128, D]))
        P.dma(bb[:], self.prow_in[l:l + 1, o + D:o + 2 * D].to_broadcast([128, D]))
        self.ln_stats = self.sb(st, f"lnst{l}_{idx}", [128, 4, 6], F32)
        self.ln_mv = self.sb(st, f"lnmv{l}_{idx}", [128, 4], F32)
        return gb, bb

    def proj_epilogue_phase(self, l, idx, srcT, wnames, last=False, xsrc=None):
        P = self.P
        if xsrc is None:
            xsrc = self.x_in if (l == 0 and idx == 0) else self.XD
        with contextlib.ExitStack() as st:
            XR = self.sb(st, f"XR{l}_{idx}", [128, NT, D], F32)
            xin = self.sb(st, f"xin{l}_{idx}", [128, 3, 512], F32)
            gb, bb = self.load_ln_params(st, l, idx)
            get = self.prefetch(l, wnames)
            k = 0
            for j in range(4):
                W = get(j)
                for t in range(NT):
                    xi = xin[:, k % 3, :]
                    k += 1
                    P.dma(xi, xsrc[t * 128:(t + 1) * 128, j * 512:(j + 1) * 512])
                    b = self.bank()
                    for kc in range(16):
                        P.mm(b[:], srcT[:, kc, t * 128:(t + 1) * 128], W[:, kc, :], start=(kc == 0), stop=(kc == 15))
                    P.stt(XR[:, t, j * 512:(j + 1) * 512], xi, ALPHA, b[:], ALU.mult, ALU.add)
            for t in range(NT):
                self.layer_norm_tile(st, t, XR[:, t, :], gb[:], bb[:], l, last)
            P.barrier()

    def layer(self, l):
        P = self.P
        with contextlib.ExitStack() as lst:
            self.pc = self.sb(lst, f"pc{l}", [128, PC_N], F32)
            P.dma(self.pc[:], self.pcol_in[l])
            self.mixer(l, lst)
            if self.stop_after in ("mixed", "mixer"):
                if self.stop_after == "mixer":
                    P.dma(self.out, self.XD, eng="sp")
                return
            self.xattn(l)
            if self.stop_after == "xattn":
                P.dma(self.out, self.XD, eng="sp")
                return
            if self.stop_after == "ffn":
                d2 = self.nc.dram_tensor("dbg_x2", [T, D], F32, kind="ExternalOutput").ap()
                P.dma(d2, self.XD, eng="sp")
            self.ffn(l, last=(l == self.nl - 1))
            P.barrier()

    def mixer(self, l, lst):
        P = self.P
        with contextlib.ExitStack() as ms:
            mixedT = self.sb(ms, f"mixedT{l}", [128, 16, T], BF16)
            with contextlib.ExitStack() as pst:
                packC = self.sb(pst, f"packC{l}", [128, 2048], F32)
                packD = self.sb(pst, f"packD{l}", [128, 2048], F32)
                Sg = packC[:, 0:1024].rearrange("p (h v) -> p h v", h=4)
                Ss = packC[:, 1024:2048].rearrange("p (g c) -> p g c", g=2)
                P.memset(packC[:], 0.0)
                P.memset(packD[:], 0.0)
                self.mixer_pass(l, ms, False, Sg, Ss, packD, None)
                P.barrier()
                dstC = self.exch(f"sc{l}", packC[:])
                dstD = self.exch(f"sd{l}", packD[:])
                with contextlib.ExitStack() as cs:
                    g2 = self.sb(cs, f"g2_{l}", [128, 2, 2048], F32)
                    gd = self.sb(cs, f"gd_{l}", [128, 8, 32], F32)
                    deff = self.sb(cs, f"deff{l}", [128, 20], F32)
                    P.dma(gd[:], dstD.rearrange("(r p) x -> p r x", p=128)[:, :, 0:32])
                    P.memset(packC[:], 0.0)
                    Sg_in = packC[:, 0:1024].rearrange("p (h v) -> p h v", h=4)
                    Ss_in = packC[:, 1024:2048].rearrange("p (h c) -> p h c", h=16)
                    for r in range(NCORES - 1):
                        gr = g2[:, r % 2, :]
                        P.dma(gr, dstC[r * 128:(r + 1) * 128, :])
                        m = self.cm[:, 8 + r:9 + r]
                        m1 = self.cm[:, 16 + r:17 + r]
                        P.ts(deff[:], gd[:, r, 0:20], m, ALU.mult, m1, ALU.add)
                        P.ts(gr, gr, m, ALU.mult)
                        for h in range(4):
                            P.stt(Sg_in[:, h, :], Sg_in[:, h, :], deff[:, h:h + 1],
                                  gr[:, 256 * h:256 * (h + 1)], ALU.mult, ALU.add)
                        P.tt(Ss_in, Ss_in, deff[:, 4:20].unsqueeze(2).to_broadcast([128, 16, 64]), ALU.mult)
                        P.tt(packC[:, 1024:2048], packC[:, 1024:2048], gr[:, 1024:2048], ALU.add)
                    P.barrier()
                Sg2, Ss2 = Sg, Ss
                self.mixer_pass(l, ms, True, Sg2, Ss2, None, mixedT)
                P.barrier()
            if self.stop_after == "mixed":
                P.dump("mixedT", mixedT[:])
                return
            self.proj_epilogue_phase(l, 0, mixedT, [f"O_{j}" for j in range(4)])

    def mixer_A(self):
        P = self.P
        with contextlib.ExitStack() as pst:
            packC = self.sb(pst, "packC0", [128, 2048], F32)
            packD = self.sb(pst, "packD0", [128, 2048], F32)
            Sg = packC[:, 0:1024].rearrange("p (h v) -> p h v", h=4)
            Ss = packC[:, 1024:2048].rearrange("p (g c) -> p g c", g=2)
            P.memset(packC[:], 0.0)
            P.memset(packD[:], 0.0)
            self.mixer_pass(0, pst, False, Sg, Ss, packD, None)
            P.dma(self.outC, packC[:])
            P.dma(self.outD, packD[:])
            P.barrier()

    def mixer_B(self):
        P = self.P
        l = 0
        with contextlib.ExitStack() as ms:
            mixedT = self.sb(ms, "mixedT0", [128, 16, T], BF16)
            with contextlib.ExitStack() as pst:
                packC = self.sb(pst, "packC0", [128, 2048], F32)
                Sg = packC[:, 0:1024].rearrange("p (h v) -> p h v", h=4)
                Ss = packC[:, 1024:2048].rearrange("p (g c) -> p g c", g=2)
                with contextlib.ExitStack() as cs:
                    g2 = self.sb(cs, "g2_0", [128, 2, 2048], F32)
                    gd = self.sb(cs, "gd_0", [128, 8, 32], F32)
                    deff = self.sb(cs, "deff0", [128, 20], F32)
                    P.dma(gd[:], self.gD_in.rearrange("(r p) x -> p r x", p=128)[:, :, 0:32])
                    P.memset(packC[:], 0.0)
                    Ss_in = packC[:, 1024:2048].rearrange("p (h c) -> p h c", h=16)
                    for r in range(NCORES - 1):
                        gr = g2[:, r % 2, :]
                        P.dma(gr, self.gC_in[r * 128:(r + 1) * 128, :])
                        m = self.cm[:, 8 + r:9 + r]
                        m1 = self.cm[:, 16 + r:17 + r]
                        P.ts(deff[:], gd[:, r, 0:20], m, ALU.mult, m1, ALU.add)
                        P.ts(gr, gr, m, ALU.mult)
                        for h in range(4):
                            P.stt(Sg[:, h, :], Sg[:, h, :], deff[:, h:h + 1],
                                  gr[:, 256 * h:256 * (h + 1)], ALU.mult, ALU.add)
                        P.tt(Ss_in, Ss_in, deff[:, 4:20].unsqueeze(2).to_broadcast([128, 16, 64]), ALU.mult)
                        P.tt(packC[:, 1024:2048], packC[:, 1024:2048], gr[:, 1024:2048], ALU.add)
                    P.barrier()
                self.mixer_pass(l, ms, True, Sg, Ss, None, mixedT)
                P.barrier()
            self.proj_epilogue_phase(l, 0, mixedT, [f"O_{j}" for j in range(4)], xsrc=self.x_in)

    def mixer_pass(self, l, ms, passB, Sg, Ss, pack, mixedT):
        P = self.P
        XT = self.XT
        tg = "B" if passB else "A"
        with contextlib.ExitStack() as st:
            names = []
            for h in range(4):
                names.append(f"G1_{h}")
                if passB:
                    names.append(f"G2_{h}")
            for g in range(2):
                names += [f"S1_{g}", f"S2_{g}"]
                if passB:
                    names.append(f"S3_{g}")
            get = self.prefetch(l, names)
            wi = 0
            wsb = self.sb(st, f"wsb{l}{tg}", [128, 512], BF16)
            P.dma(wsb[:], self.ws_in[l], eng="pool")
            WS = wsb[:].rearrange("p (k c) -> p k c", k=16)
            alrT = self.sb(st, f"alrT{l}{tg}", [16, T], F32)
            wgate = self.sb(st, f"wgate{l}{tg}", [16, 512], F32)
            P.dma(wgate[:], self.wgate_in[l])
            for half in range(2):
                b = self.bank()
                for kc in range(16):
                    P.mm(b[0:16, :], WS[:, kc, 0:16], XT[:, kc, half * 512:(half + 1) * 512], start=(kc == 0), stop=(kc == 15))
                P.copy(alrT[:, half * 512:(half + 1) * 512], b[0:16, :], eng="act")
            dt_ = self.sb(st, f"dt{l}{tg}", [128, NT, 16], F32)
            dtA = self.sb(st, f"dtA{l}{tg}", [128, NT, 16], F32)
            rows = self.sb(st, f"rows{l}{tg}", [128, 48], F32)
            P.dma(rows[:], self.prow_in[l:l + 1, PR_DTB:PR_DTB + 48].to_broadcast([128, 48]))
            P.act(rows[:, 16:32], rows[:, 16:32], AF.Exp)
            P.ts(rows[:, 16:32], rows[:, 16:32], -1.0, ALU.mult)
            b = self.bank()
            for t in range(NT):
                for kc in range(16):
                    P.mm(b[:, t * 16:(t + 1) * 16], XT[:, kc, t * 128:(t + 1) * 128], WS[:, kc, 16:32], start=(kc == 0), stop=(kc == 15))
            b3 = b[:, 0:NT * 16].rearrange("p (t h) -> p t h", t=NT)
            P.tt(dt_[:], b3, rows[:, 0:16].unsqueeze(1).to_broadcast([128, NT, 16]), ALU.add)
            P.act(dt_[:], dt_[:], AF.Exp)
            P.act(dt_[:], dt_[:], AF.Ln, bias=self.c_one, scale=1.0)
            P.tt(dtA[:], dt_[:], rows[:, 16:32].unsqueeze(1).to_broadcast([128, NT, 16]), ALU.mult)

            with contextlib.ExitStack() as gs:
                f1 = self.sb(gs, f"gf1{l}{tg}", [128, T], F32)
                cs = self.sb(gs, f"gcs{l}{tg}", [128, T], F32)
                eend = self.sb(gs, f"geend{l}{tg}", [128, T], F32)
                kendT = self.sb(gs, f"kendT{l}{tg}", [128, T], BF16)
                kend_tm = self.sb(gs, f"kendtm{l}{tg}", [128, NT, 128], BF16)
                v_tm = self.sb(gs, f"vtm{l}{tg}", [128, NT, 256], BF16)
                dec = self.sb(gs, f"gdec{l}{tg}", [128, NT + 1], F32)
                if passB:
                    eb = self.sb(gs, f"geb{l}", [128, T], F32)
                    einv = self.sb(gs, f"geinv{l}", [128, T], F32)
                    kinvT = self.sb(gs, f"kinvT{l}", [128, T], BF16)
                    qdecT = self.sb(gs, f"qdecT{l}", [128, T], BF16)
                    sog = self.sb(gs, f"sog{l}", [128, NT, 256], BF16)
                    mixh = self.sb(gs, f"mixh{l}", [128, 2, 256], BF16)
                    Sbf = self.sb(gs, f"gSbf{l}", [128, 256], BF16)
                    sTm = self.sb(gs, f"sTm{l}", [128, 2, 128], BF16)
                    on = self.sb(gs, f"gon{l}", [128, 2, 256], F32)
                    junk = self.sb(gs, f"gjunk{l}", [128, 256], F32)
                    ssq = self.sb(gs, f"gssq{l}", [128, 2, 4], F32)
                    gnw = self.sb(gs, f"gnw{l}", [128, 1024], F32)
                    P.dma(gnw[:], self.prow_in[l:l + 1, PR_GNW:PR_GNW + 1024].to_broadcast([128, 1024]))
                cs3 = cs[:].rearrange("p (t c) -> p t c", c=128)
                for h in range(4):
                    W1 = get(wi); wi += 1
                    for half in range(2):
                        sl = slice(half * 512, (half + 1) * 512)
                        b = self.bank()
                        P.mm(b[:], wgate[:, h * 128:(h + 1) * 128], alrT[:, sl])
                        P.act(f1[:, sl], b[:], AF.Exp, bias=self.pc[:, PC_NEGB + h:PC_NEGB + h + 1], scale=-1.0)
                    P.act(f1[:], f1[:], AF.Ln, bias=self.c_one, scale=1.0)
                    P.scan(cs[:], self.reset, f1[:], 0.0, ALU.mult, ALU.add)
                    P.act(dec[:, 0:NT], cs3[:, :, 127], AF.Exp, scale=-1.0 / 16.0)
                    P.tt(f1[:].rearrange("p (t c) -> p t c", c=128), cs3, cs3[:, :, 127:128].to_broadcast([128, NT, 128]), ALU.subtract)
                    P.act(eend[:], f1[:], AF.Exp, scale=1.0 / 16.0)
                    if not passB:
                        P.reduce(dec[:, NT:NT + 1], cs3[:, :, 127], ALU.add)
                        P.act(pack[:, h:h + 1], dec[:, NT:NT + 1], AF.Exp, scale=-1.0 / 16.0)
                    else:
                        P.act(eb[:], cs[:], AF.Exp, bias=self.c_lnq, scale=-1.0 / 16.0)
                        P.act(einv[:], cs[:], AF.Exp, scale=1.0 / 16.0)
                    for half in range(2):
                        sl = slice(half * 512, (half + 1) * 512)
                        b = self.bank()
                        for kc in range(16):
                            P.mm(b[:], W1[:, kc, 0:128], XT[:, kc, sl], start=(kc == 0), stop=(kc == 15))
                        P.tt(kendT[:, sl], b[:], eend[:, sl], ALU.mult)
                        if passB:
                            P.tt(kinvT[:, sl], b[:], einv[:, sl], ALU.mult)
                            b2 = self.bank()
                            for kc in range(16):
                                P.mm(b2[:], W1[:, kc, 384:512], XT[:, kc, sl], start=(kc == 0), stop=(kc == 15))
                            P.tt(qdecT[:, sl], b2[:], eb[:, sl], ALU.mult)
                    for tq in range(2):
                        b = self.bank()
                        for j in range(4):
                            t = 4 * tq + j
                            P.mm(b[:, j * 128:(j + 1) * 128], kendT[:, t * 128:(t + 1) * 128], self.identB[:])
                        P.copy(kend_tm[:, 4 * tq:4 * tq + 4, :], b[:].rearrange("p (j c) -> p j c", j=4), eng="act")
                    for t2 in range(NT // 2):
                        b = self.bank()
                        for j in range(2):
                            t = 2 * t2 + j
                            for kc in range(16):
                                P.mm(b[:, j * 256:(j + 1) * 256], XT[:, kc, t * 128:(t + 1) * 128], W1[:, kc, 128:384], start=(kc == 0), stop=(kc == 15))
                        P.copy(v_tm[:, 2 * t2:2 * t2 + 2, :], b[:].rearrange("p (j c) -> p j c", j=2), eng=("act" if t2 % 2 else "dve"))
                    if passB:
                        W2 = get(wi); wi += 1
                        for t2 in range(NT // 2):
                            b = self.bank()
                            for j in range(2):
                                t = 2 * t2 + j
                                for kc in range(16):
                                    P.mm(b[:, j * 256:(j + 1) * 256], XT[:, kc, t * 128:(t + 1) * 128], W2[:, kc, :], start=(kc == 0), stop=(kc == 15))
                            P.act(sog[:, 2 * t2:2 * t2 + 2, :], b[:].rearrange("p (j c) -> p j c", j=2), AF.Silu)
                    for t in range(NT):
                        tsl = slice(t * 128, (t + 1) * 128)
                        if passB:
                            P.copy(Sbf[:], Sg[:, h, :], eng="act")
                            b = self.bank()
                            P.mm(b[:, 0:128], kinvT[:, tsl], qdecT[:, tsl])
                            P.tt(sTm[:, t % 2, :], b[:, 0:128], self.causal, ALU.mult)
                            ob = self.bank()
                            P.mm(ob[:, 0:256], sTm[:, t % 2, :], v_tm[:, t, :], start=True, stop=False)
                            P.mm(ob[:, 0:256], qdecT[:, tsl], Sbf[:], start=False, stop=True)
                            sq = ssq[:, t % 2, :]
                            P.act(junk[:], ob[:, 0:256], AF.Square, accum_out=sq[:, 0:1])
                            P.act(sq[:, 1:2], sq[:, 0:1], AF.Ln, bias=self.c_rmseps, scale=1.0 / 256.0)
                            P.act(sq[:, 2:3], sq[:, 1:2], AF.Exp, scale=-0.5)
                            P.stt(on[:, t % 2, :], ob[:, 0:256], sq[:, 2:3], gnw[:, h * 256:(h + 1) * 256], ALU.mult, ALU.mult)
                            P.tt(mixh[:, t % 2, :], on[:, t % 2, :], sog[:, t, :], ALU.mult, eng="pool")
                            self.to_mixedT_tile(mixh[:, t % 2, :], mixedT, 2 * h, 2, t)
                        cb = self.bank()
                        P.mm(cb[:, 0:256], kend_tm[:, t, :], v_tm[:, t, :])
                        P.stt(Sg[:, h, :], Sg[:, h, :], dec[:, t:t + 1], cb[:, 0:256], ALU.mult, ALU.add)
                P.barrier()

            with contextlib.ExitStack() as ss:
                pre = self.sb(ss, f"pre{l}{tg}", [128, 2, T + 4], BF16)
                cvT = self.sb(ss, f"cvT{l}{tg}", [128, 6, T], BF16)
                xs_tm = self.sb(ss, f"xstm{l}{tg}", [128, NT, 512], BF16)
                B_tm = self.sb(ss, f"Btm{l}{tg}", [128, NT, 128], BF16)
                dg = self.sb(ss, f"dg{l}{tg}", [128, 2, 4, 128], BF16)
                xdt = self.sb(ss, f"xdt{l}{tg}", [128, 2, 512], BF16)
                xend = self.sb(ss, f"xend{l}{tg}", [128, 2, 512], BF16)
                sm = self.sb(ss, f"ssm{l}{tg}", [128, 2, 64], F32)
                dtot = self.sb(ss, f"dtot{l}{tg}", [128, 16], F32)
                P.memset(dtot[:], 0.0)
                if passB:
                    sz = self.sb(ss, f"sz{l}", [128, NT, 512], BF16)
                    mixg = self.sb(ss, f"mixg{l}", [128, 2, 512], BF16)
                    Sbf = self.sb(ss, f"sSbf{l}", [128, 512], BF16)
                    cbm = self.sb(ss, f"cbm{l}", [128, 2, 128], F32)
                    tri = self.sb(ss, f"tri{l}", [128, 8, 128], F32)
                    Lm = self.sb(ss, f"Lm{l}", [128, 8, 128], BF16)
                    Wm = self.sb(ss, f"Wm{l}", [128, 8, 128], BF16)
                    ytmp = self.sb(ss, f"ytmp{l}", [128, 1, 512], F32)
                    y2 = self.sb(ss, f"y2{l}", [128, 1, 512], F32)
                    junk = self.sb(ss, f"sjunk{l}", [128, 512], F32)
                    ssq = self.sb(ss, f"sssq{l}", [128, 2, 4], F32)
                    snw = self.sb(ss, f"snw{l}", [128, 1024], F32)
                    P.dma(snw[:], self.prow_in[l:l + 1, PR_SNW:PR_SNW + 1024].to_broadcast([128, 1024]))
                for g in range(2):
                    W1 = get(wi); wi += 1
                    W2 = get(wi); wi += 1
                    nch = 6 if passB else 5
                    for ci in range(nch):
                        Wc = W1[:, :, ci * 128:(ci + 1) * 128] if ci < 4 else W2[:, :, (ci - 4) * 128:(ci - 3) * 128]
                        chan = (4 * g + ci) if ci < 4 else (8 + g if ci == 4 else 10 + g)
                        b = self.bank()
                        for kc in range(16):
                            P.mm(b[:, 0:3], Wc[:, kc, :], self.XTh[:, kc, 0:3], start=(kc == 0), stop=(kc == 15))
                        P.copy(pre[:, ci % 2, 0:3], b[:, 0:3], eng="act")
                        for half in range(2):
                            b = self.bank()
                            for kc in range(16):
                                P.mm(b[:], Wc[:, kc, :], XT[:, kc, half * 512:(half + 1) * 512], start=(kc == 0), stop=(kc == 15))
                            P.copy(pre[:, ci % 2, 3 + half * 512:3 + (half + 1) * 512], b[:], eng=("act" if half else "dve"))
                        dgi = dg[:, ci % 2, :, :]
                        for j in range(4):
                            P.ts(dgi[:, j, :], self.identF, self.pc[:, PC_CW + chan * 4 + j:PC_CW + chan * 4 + j + 1], ALU.mult, eng="pool")
                        for half in range(2):
                            b = self.bank()
                            for j in range(4):
                                P.mm(b[:], dgi[:, j, :], pre[:, ci % 2, half * 512 + j:half * 512 + j + 512], start=(j == 0), stop=(j == 3))
                            P.act(cvT[:, ci, half * 512:(half + 1) * 512], b[:], AF.Silu, bias=self.pc[:, PC_CB + chan:PC_CB + chan + 1], scale=1.0)
                    for t in range(NT):
                        b = self.bank()
                        for ci in range(4):
                            P.mm(b[:, ci * 128:(ci + 1) * 128], cvT[:, ci, t * 128:(t + 1) * 128], self.identB[:])
                        P.copy(xs_tm[:, t, :], b[:], eng=("act" if t % 2 else "dve"))
                    for tq in range(2):
                        b = self.bank()
                        for j in range(4):
                            t = 4 * tq + j
                            P.mm(b[:, j * 128:(j + 1) * 128], cvT[:, 4, t * 128:(t + 1) * 128], self.identB[:])
                        P.copy(B_tm[:, 4 * tq:4 * tq + 4, :], b[:].rearrange("p (j c) -> p j c", j=4), eng="act")
                    if passB:
                        W3 = get(wi); wi += 1
                        for t in range(NT):
                            b = self.bank()
                            for kc in range(16):
                                P.mm(b[:], XT[:, kc, t * 128:(t + 1) * 128], W3[:, kc, :], start=(kc == 0), stop=(kc == 15))
                            P.act(sz[:, t, :], b[:], AF.Silu)
                    hs = slice(8 * g, 8 * g + 8)
                    for t in range(NT):
                        tsl = slice(t * 128, (t + 1) * 128)
                        k2 = t % 2
                        b = self.bank()
                        P.mm(b[:, 0:8], self.causal, dtA[:, t, hs])
                        P.mm(b[:, 8:16], self.ones, dtA[:, t, hs])
                        P.copy(sm[:, k2, 0:16], b[:, 0:16], eng="act")
                        P.tt(sm[:, k2, 32:40], sm[:, k2, 8:16], sm[:, k2, 0:8], ALU.subtract)
                        P.act(sm[:, k2, 16:24], sm[:, k2, 0:8], AF.Exp)
                        P.act(sm[:, k2, 32:40], sm[:, k2, 32:40], AF.Exp)
                        P.act(sm[:, k2, 48:56], sm[:, k2, 8:16], AF.Exp)
                        if not passB:
                            P.tt(dtot[:, hs], dtot[:, hs], sm[:, k2, 8:16], ALU.add)
                        xs3 = xs_tm[:, t, :].rearrange("p (h c) -> p h c", h=8)
                        xdt3 = xdt[:, k2, :].rearrange("p (h c) -> p h c", h=8)
                        P.tt(xdt3, xs3, dt_[:, t, hs].unsqueeze(2).to_broadcast([128, 8, 64]), ALU.mult)
                        P.tt(xend[:, k2, :].rearrange("p (h c) -> p h c", h=8), xdt3,
                             sm[:, k2, 32:40].unsqueeze(2).to_broadcast([128, 8, 64]), ALU.mult, eng="pool")
                        if passB:
                            P.copy(Sbf[:], Ss[:, g, :], eng="act")
                            b = self.bank()
                            P.mm(b[:, 0:128], cvT[:, 4, tsl], cvT[:, 5, tsl])
                            P.tt(cbm[:, k2, :], b[:, 0:128], self.causal, ALU.mult)
                            P.tt(tri[:], self.causal.unsqueeze(1).to_broadcast([128, 8, 128]),
                                 dtA[:, t, hs].unsqueeze(2).to_broadcast([128, 8, 128]), ALU.mult)
                            for hq in range(2):
                                sb_ = self.bank()
                                for j in range(4):
                                    P.mm(sb_[:, j * 128:(j + 1) * 128], self.strict, tri[:, 4 * hq + j, :])
                                P.act(Lm[:, 4 * hq:4 * hq + 4, :], sb_[:].rearrange("p (j c) -> p j c", j=4), AF.Exp)
                            P.tt(Wm[:], Lm[:], cbm[:, k2, :].unsqueeze(1).to_broadcast([128, 8, 128]), ALU.mult)
                            yb = self.bank()
                            for hh in range(8):
                                P.mm(yb[:, hh * 64:(hh + 1) * 64], Wm[:, hh, :], xdt[:, k2, hh * 64:(hh + 1) * 64])
                            ob = self.bank()
                            P.mm(ob[:], cvT[:, 5, tsl], Sbf[:])
                            yt3 = ytmp[:, 0, :].rearrange("p (h c) -> p h c", h=8)
                            P.tt(yt3, ob[:].rearrange("p (h c) -> p h c", h=8),
                                 sm[:, k2, 16:24].unsqueeze(2).to_broadcast([128, 8, 64]), ALU.mult)
                            P.tt(ytmp[:, 0, :], ytmp[:, 0, :], yb[:], ALU.add)
                            y23 = y2[:, 0, :].rearrange("p (h c) -> p h c", h=8)
                            P.tt(y23, xs3, rows[:, 32 + 8 * g:40 + 8 * g].unsqueeze(2).to_broadcast([128, 8, 64]), ALU.mult, eng="pool")
                            P.tt(y2[:, 0, :], y2[:, 0, :], ytmp[:, 0, :], ALU.add, eng="pool")
                            P.tt(y2[:, 0, :], y2[:, 0, :], sz[:, t, :], ALU.mult, eng="pool")
                            sq = ssq[:, k2, :]
                            P.act(junk[:], y2[:, 0, :], AF.Square, accum_out=sq[:, 0:1])
                            P.act(sq[:, 1:2], sq[:, 0:1], AF.Ln, bias=self.c_rmseps, scale=1.0 / 512.0)
                            P.act(sq[:, 2:3], sq[:, 1:2], AF.Exp, scale=-0.5)
                            P.stt(mixg[:, k2, :], y2[:, 0, :], sq[:, 2:3], snw[:, g * 512:(g + 1) * 512], ALU.mult, ALU.mult)
                            self.to_mixedT_tile(mixg[:, k2, :], mixedT, 8 + 4 * g, 4, t)
                        cb = self.bank()
                        P.mm(cb[:], B_tm[:, t, :], xend[:, k2, :])
                        S3 = Ss[:, g, :].rearrange("p (h c) -> p h c", h=8)
                        P.tt(S3, S3, sm[:, k2, 48:56].unsqueeze(2).to_broadcast([128, 8, 64]), ALU.mult)
                        P.tt(Ss[:, g, :], Ss[:, g, :], cb[:], ALU.add)
                if not passB:
                    P.act(pack[:, 4:20], dtot[:], AF.Exp)
                P.barrier()

    def to_mixedT_tile(self, src, mixedT, c0, nch, t):
        P = self.P
        b = self.bank()
        for ci in range(nch):
            P.mm(b[:, ci * 128:(ci + 1) * 128], src[:, ci * 128:(ci + 1) * 128], self.identB[:])
        P.copy(mixedT[:, c0:c0 + nch, t * 128:(t + 1) * 128], b[:, 0:nch * 128].rearrange("p (j c) -> p j c", j=nch),
               eng=("act" if t % 2 else "dve"))

    def to_mixedT(self, src_tm, mixedT, c0, nch):
        P = self.P
        for ci in range(nch):
            for tq in range(2):
                b = self.bank()
                for j in range(4):
                    t = 4 * tq + j
                    P.mm(b[:, j * 128:(j + 1) * 128], src_tm[:, t, ci * 128:(ci + 1) * 128], self.identB[:])
                P.copy(mixedT[:, c0 + ci, 4 * tq * 128:(4 * tq + 4) * 128], b[:], eng=("act" if tq else "dve"))

    def xattn(self, l, last=False):
        P = self.P
        XT = self.XT
        with contextlib.ExitStack() as xs:
            oT = self.sb(xs, f"oT{l}", [128, 16, T], BF16)
            with contextlib.ExitStack() as st:
                memF = self.sb(st, f"memF{l}", [128, 16, NMEM], F32)
                memT = self.sb(st, f"memTb{l}", [128, 16, NMEM], BF16)
                kT = self.sb(st, f"kT{l}", [128, 16, NMEM], BF16)
                vx = self.sb(st, f"vx{l}", [128, 2, D], BF16)
                qT = self.sb(st, f"qT{l}", [128, 4, T], BF16)
                sc = self.sb(st, f"sc{l}", [128, 2, 4], F32)
                pe_ = self.sb(st, f"pexp{l}", [128, 2, NMEM], F32)
                pn = self.sb(st, f"pn{l}", [128, 2, NMEM], BF16)
                pT = self.sb(st, f"pT{l}", [128, 2, T], BF16)
                P.dma(memF[:], self.memT_in)
                P.copy(memT[:, 0:8, :], memF[:, 0:8, :], eng="act")
                P.copy(memT[:, 8:16, :], memF[:, 8:16, :], eng="dve")
                names = [f"K_{j}" for j in range(4)] + [f"V_{j}" for j in range(4)] + [f"Q_{j}" for j in range(4)]
                get = self.prefetch(l, names)
                for j in range(4):
                    W = get(j)
                    for cc in range(4):
                        b = self.bank()
                        for kc in range(16):
                            P.mm(b[:, 0:NMEM], W[:, kc, cc * 128:(cc + 1) * 128], memT[:, kc, :], start=(kc == 0), stop=(kc == 15))
                        P.copy(kT[:, 4 * j + cc, :], b[:, 0:NMEM], eng=("act" if cc % 2 else "dve"))
                for j in range(4):
                    W = get(4 + j)
                    for mt in range(2):
                        b = self.bank()
                        for kc in range(16):
                            P.mm(b[:], memT[:, kc, mt * 128:(mt + 1) * 128], W[:, kc, :], start=(kc == 0), stop=(kc == 15))
                        P.copy(vx[:, mt, j * 512:(j + 1) * 512], b[:], eng=("act" if mt else "dve"))
                scale = 512.0 ** -0.5
                for h in range(4):
                    W = get(8 + h)
                    for cc in range(4):
                        for half in range(2):
                            b = self.bank()
                            for kc in range(16):
                                P.mm(b[:], W[:, kc, cc * 128:(cc + 1) * 128], XT[:, kc, half * 512:(half + 1) * 512], start=(kc == 0), stop=(kc == 15))
                            P.copy(qT[:, cc, half * 512:(half + 1) * 512], b[:], eng=("act" if (cc + half) % 2 else "dve"))
                    for t in range(NT):
                        tsl = slice(t * 128, (t + 1) * 128)
                        k2 = t % 2
                        b = self.bank()
                        for cc in range(4):
                            P.mm(b[:, 0:NMEM], qT[:, cc, tsl], kT[:, 4 * h + cc, :], start=(cc == 0), stop=(cc == 3))
                        P.reduce(sc[:, k2, 0:1], b[:, 0:NMEM], ALU.max)
                        P.ts(sc[:, k2, 1:2], sc[:, k2, 0:1], -scale, ALU.mult)
                        P.act(pe_[:, k2, :], b[:, 0:NMEM], AF.Exp, bias=sc[:, k2, 1:2], scale=scale, accum_out=sc[:, k2, 2:3])
                        P.op("dve", lambda e, o=sc[:, k2, 3:4], i=sc[:, k2, 2:3]: e.reciprocal(out=o, in_=i), [sc[:, k2, 2:3]], [sc[:, k2, 3:4]])
                        P.ts(pn[:, k2, :], pe_[:, k2, :], sc[:, k2, 3:4], ALU.mult)
                        b2 = self.bank()
                        for mt in range(2):
                            P.mm(b2[:, mt * 128:(mt + 1) * 128], pn[:, k2, mt * 128:(mt + 1) * 128], self.identB[:])
                        P.copy(pT[:, :, tsl], b2[:, 0:256].rearrange("p (m c) -> p m c", m=2), eng="act")
                    for cc in range(4):
                        for half in range(2):
                            b = self.bank()
                            for mt in range(2):
                                P.mm(b[:], vx[:, mt, (4 * h + cc) * 128:(4 * h + cc + 1) * 128], pT[:, mt, half * 512:(half + 1) * 512], start=(mt == 0), stop=(mt == 1))
                            P.copy(oT[:, 4 * h + cc, half * 512:(half + 1) * 512], b[:], eng=("act" if (cc + half) % 2 else "dve"))
                P.barrier()
            self.proj_epilogue_phase(l, 1, oT, [f"WO_{j}" for j in range(4)], last=last)

    def ffn(self, l, last, standalone=False):
        P = self.P
        XT = self.XT
        if not standalone:
            self.halo_exchange(l, "f", 2)
            if l + 1 < self.nl:
                self.gather_weights(l + 1)
        with contextlib.ExitStack() as fs:
            XR = self.sb(fs, f"XRf{l}", [128, NT, D], F32)
            for t in range(NT):
                P.dma(XR[:, t, :], (self.x_in if standalone else self.XD)[t * 128:(t + 1) * 128, :])
            with contextlib.ExitStack() as st:
                hT = self.sb(st, f"hT{l}", [128, 12, T], BF16)
                pre = self.sb(st, f"fpre{l}", [128, 2, T + 4], BF16)
                dg = self.sb(st, f"fdg{l}", [128, 2, 3, 128], BF16)
                gl = self.sb(st, f"fgl{l}", [128, 2, T], F32)
                names = []
                ui = 0
                for gi, nu in enumerate(U_GROUPS):
                    names += [f"U_{ui + i}" for i in range(nu)]
                    names += [f"D_{gi}_{j}" for j in range(4)]
                    ui += nu
                get = self.prefetch(l, names)
                wi = 0
                ui = 0
                first = True
                for gi, nu in enumerate(U_GROUPS):
                    nk = 2 * nu
                    for i in range(nu):
                        W = get(wi); wi += 1
                        for c2 in range(2):
                            chan = 2 * (ui + i) + c2
                            kloc = 2 * i + c2
                            pr = pre[:, c2, :]
                            Wg_ = W[:, :, c2 * 128:(c2 + 1) * 128]
                            Wv_ = W[:, :, 256 + c2 * 128:256 + (c2 + 1) * 128]
                            b = self.bank()
                            for kc in range(16):
                                P.mm(b[:, 0:2], Wg_[:, kc, :], self.XTh[:, kc, 0:2], start=(kc == 0), stop=(kc == 15))
                            P.copy(pr[:, 0:2], b[:, 0:2], eng="act")
                            for half in range(2):
                                b = self.bank()
                                for kc in range(16):
                                    P.mm(b[:], Wg_[:, kc, :], XT[:, kc, half * 512:(half + 1) * 512], start=(kc == 0), stop=(kc == 15))
                                P.copy(pr[:, 2 + half * 512:2 + (half + 1) * 512], b[:], eng=("act" if half else "dve"))
                            for j in range(3):
                                P.ts(dg[:, c2, j, :], self.identF, self.pc[:, PC_FW + chan * 3 + j:PC_FW + chan * 3 + j + 1], ALU.mult, eng="pool")
                            for half in range(2):
                                b = self.bank()
                                for j in range(3):
                                    P.mm(b[:], dg[:, c2, j, :], pr[:, half * 512 + j:half * 512 + j + 512], start=(j == 0), stop=(j == 2))
                                P.act(gl[:, c2, half * 512:(half + 1) * 512], b[:], AF.Gelu, bias=self.pc[:, PC_FB + chan:PC_FB + chan + 1], scale=1.0)
                            for half in range(2):
                                b = self.bank()
                                for kc in range(16):
                                    P.mm(b[:], Wv_[:, kc, :], XT[:, kc, half * 512:(half + 1) * 512], start=(kc == 0), stop=(kc == 15))
                                P.tt(hT[:, kloc, half * 512:(half + 1) * 512], b[:], gl[:, c2, half * 512:(half + 1) * 512], ALU.mult)
                    for j in range(4):
                        Wd = get(wi); wi += 1
                        for t in range(NT):
                            b = self.bank()
                            for kc in range(nk):
                                P.mm(b[:], hT[:, kc, t * 128:(t + 1) * 128], Wd[:, kc, :], start=(kc == 0), stop=(kc == nk - 1))
                            xr = XR[:, t, j * 512:(j + 1) * 512]
                            if first:
                                P.stt(xr, xr, ALPHA, b[:], ALU.mult, ALU.add)
                            else:
                                P.tt(xr, xr, b[:], ALU.add)
                    first = False
                    ui += nu
                P.barrier()
            with contextlib.ExitStack() as st:
                gb, bb = self.load_ln_params(st, l, 2)
                for t in range(NT):
                    self.layer_norm_tile(st, t, XR[:, t, :], gb[:], bb[:], l, last)
                P.barrier()
        if not last:
            self.halo_exchange(l, "m", 3)

    def halo_exchange(self, l, tag, n):
        P = self.P
        with contextlib.ExitStack() as st:
            hsrc = self.sb(st, f"hsrc{l}{tag}", [128, 2048], F32)
            hall = self.sb(st, f"hall{l}{tag}", [128, 8, 64], F32)
            acc = self.sb(st, f"hacc{l}{tag}", [128, 64], F32)
            P.memset(hsrc[:], 0.0)
            P.copy(hsrc[:, 0:16 * n].rearrange("p (c j) -> p c j", j=n), self.XT[:, :, T - n:T])
            dst = self.exch(f"h{l}{tag}", hsrc[:])
            P.dma(hall[:], dst.rearrange("(r p) x -> p r x", p=128)[:, :, 0:64])
            P.memset(acc[:], 0.0)
            for r in range(NCORES - 1):
                P.stt(acc[:], hall[:, r, :], self.cm[:, r:r + 1], acc[:], ALU.mult, ALU.add)
            P.copy(self.XTh[:, :, 0:n], acc[:, 0:16 * n].rearrange("p (c j) -> p c j", j=n))
            P.barrier()


_CACHE = {}


def make_in_maps(inp, layers, cc=True, wl_dev=None):
    x = inp["x"][0]
    mem = inp["mem"][0]
    memT = np.ascontiguousarray(mem.T.reshape(16, 128, NMEM).transpose(1, 0, 2))
    consts = host_consts()
    Ws = [host_weights(inp, l) for l in layers]
    smalls = [host_small(inp, l) for l in layers]
    pcol = np.stack([s[0] for s in smalls])
    prow = np.concatenate([s[1] for s in smalls], axis=0)
    wgate = np.stack([s[2] for s in smalls])
    wsblk = np.stack([host_sblock(inp, l) for l in layers])
    maps = []
    for c in range(NCORES):
        cm = np.zeros((128, 24), np.float32)
        for r in range(8):
            cm[:, r] = 1.0 if r == c - 1 else 0.0
            cm[:, 8 + r] = 1.0 if r < c else 0.0
            cm[:, 16 + r] = 0.0 if r < c else 1.0
        xh = np.zeros((3, D), np.float32)
        if c > 0:
            xh = x[c * T - 3:c * T]
        xhT = np.ascontiguousarray(xh.T.reshape(16, 128, 3).transpose(1, 0, 2))
        m = {"x": np.ascontiguousarray(x[c * T:(c + 1) * T]), "xhalo": xhT, "memT": memT, "consts": consts,
             "cmask": cm, "pcol": pcol, "prow": prow, "wgate": wgate, "wsblk": wsblk}
        for i in range(len(layers)):
            if cc:
                m[f"wsh{i}"] = np.ascontiguousarray(Ws[i][16 * c:16 * (c + 1)])
            else:
                m[f"wsh{i}"] = np.ascontiguousarray(Ws[i][:, :wl_dev])
        maps.append(m)
    return maps


WL_A = (0, 90112)
WL_B = (0, 253952)
WL_C = (253952, WL)


def _halo(xfull, c, n):
    xh = np.zeros((3, D), np.float32)
    if c > 0:
        xh[0:n] = xfull[c * T - n:c * T]
    return np.ascontiguousarray(xh.T.reshape(16, 128, 3).transpose(1, 0, 2))


def _cmask(c):
    cm = np.zeros((128, 24), np.float32)
    for r in range(8):
        cm[:, r] = 1.0 if r == c - 1 else 0.0
        cm[:, 8 + r] = 1.0 if r < c else 0.0
        cm[:, 16 + r] = 0.0 if r < c else 1.0
    return cm


def _get_prog(kind):
    if kind not in _CACHE:
        lo, hi = {"A": WL_A, "B": WL_B, "C": WL_C}[kind]
        _CACHE[kind] = Builder(1, cc=False, wl_dev=hi - lo, kind=kind, wl_base=lo).build()
    return _CACHE[kind]


def kernel(**inputs):
    inp = {k: np.asarray(v) for k, v in inputs.items()}
    x = np.ascontiguousarray(inp["x"][0])
    mem = inp["mem"][0]
    memT = np.ascontiguousarray(mem.T.reshape(16, 128, NMEM).transpose(1, 0, 2))
    consts = host_consts()
    cms = [_cmask(c) for c in range(NCORES)]
    cores = list(range(NCORES))
    for l in range(DEPTH):
        Wl = host_weights(inp, l)
        pc, pr, wg = host_small(inp, l)
        common = {"memT": memT, "consts": consts, "pcol": pc[None], "prow": pr, "wgate": wg[None],
                  "wsblk": host_sblock(inp, l)[None]}

        def maps(kind, xfull, n, extra=None):
            lo, hi = {"A": WL_A, "B": WL_B, "C": WL_C}[kind]
            wsl = np.ascontiguousarray(Wl[:, lo:hi])
            out = []
            for c in cores:
                m = dict(common)
                m["x"] = np.ascontiguousarray(xfull[c * T:(c + 1) * T])
                m["xhalo"] = _halo(xfull, c, n)
                m["cmask"] = cms[c]
                m["wsh0"] = wsl
                if extra:
                    m.update(extra)
                out.append(m)
            return out

        ra = run_bass_kernel_spmd(_get_prog("A"), maps("A", x, 3), core_ids=cores)
        gC = np.concatenate([ra.results[c]["outC"] for c in cores], axis=0)
        gD = np.concatenate([ra.results[c]["outD"] for c in cores], axis=0)
        rb = run_bass_kernel_spmd(_get_prog("B"), maps("B", x, 3, {"gC": gC, "gD": gD}), core_ids=cores)
        x2 = np.concatenate([rb.results[c]["out"] for c in cores], axis=0)
        rc = run_bass_kernel_spmd(_get_prog("C"), maps("C", x2, 2), core_ids=cores)
        x = np.concatenate([rc.results[c]["out"] for c in cores], axis=0)
    return x.reshape(1, SEQ, D).astype(np.float32)
```

```python
import contextlib
import numpy as np
import concourse.bass as bass
import concourse.mybir as mybir
from concourse.bass_utils import run_bass_kernel_spmd

F32 = mybir.dt.float32
BF16 = mybir.dt.bfloat16
AF = mybir.ActivationFunctionType
ALU = mybir.AluOpType
AX = mybir.AxisListType

NCORES = 8
DEPTH = 4
D = 2048
T = 1024
NT = T // 128
SEQ = 8192
D_IN = 5664
D_FF = 5632
NMEM = 256
ALPHA = (2.0 * DEPTH) ** 0.25
LN_EPS = 1e-5
RMS_EPS = 1e-6
ENGS = ("pe", "act", "dve", "pool", "sp")


def _prod(xs):
    r = 1
    for x in xs:
        r *= int(x)
    return r


class Prog:
    def __init__(self, nc, stack):
        self.nc = nc
        self.stack = stack
        self.q = {e: [] for e in ENGS}
        self.cnt = {e: 0 for e in ENGS}
        self.sems = {}
        for e in ENGS:
            self.sems[e] = stack.enter_context(nc.semaphore("s_" + e))
        self.seen = {e: {} for e in ENGS}
        self.recs = {}
        self.dmacnt = {}
        self.nops = 0
        self.dumps = []

    def region(self, ap):
        t = ap.tensor
        name = t.name
        a = ap.ap
        off = int(ap.offset)
        row = _prod(t.shape[1:]) if len(t.shape) > 1 else 1
        p0 = off // row
        f0 = off % row
        pe = 0
        fe = 0
        for s_, c_ in a:
            s_, c_ = abs(int(s_)), int(c_)
            pe += (c_ - 1) * (s_ // row)
            fe += (c_ - 1) * (s_ % row)
        if f0 + fe >= row:
            return name, p0, p0 + pe + (f0 + fe) // row, 0, row - 1
        return name, p0, p0 + pe, f0, f0 + fe

    def _sem(self, key, dma=True):
        if key not in self.sems:
            self.sems[key] = self.stack.enter_context(self.nc.semaphore("d_" + str(key)[:40]))
            if dma:
                self.dmacnt[key] = 0
        return self.sems[key]

    def op(self, eng, fn, reads=(), writes=(), dma_key=None, own_sem=None):
        waits = {}
        regs_r = [self.region(a) for a in reads]
        regs_w = [self.region(a) for a in writes]

        def conflicts(reg, is_write):
            name, p0, p1, f0, f1 = reg
            for r in self.recs.get(name, ()):
                (q0, q1, g0, g1, key, val, w, reng) = r
                if not (is_write or w):
                    continue
                if q1 < p0 or p1 < q0 or g1 < f0 or f1 < g0:
                    continue
                if reng == "pe" and eng == "pe":
                    continue
                if key in self.dmacnt:
                    val = 16 * self.dmacnt[key]
                if waits.get(key, 0) < val:
                    waits[key] = val

        for rg in regs_r:
            conflicts(rg, False)
        for rg in regs_w:
            conflicts(rg, True)
        wl = []
        for key, val in waits.items():
            if self.seen[eng].get(key, 0) >= val:
                continue
            self.seen[eng][key] = val
            wl.append((self.sems[key], val))
        if dma_key is not None:
            sem = self._sem(dma_key)
            self.dmacnt[dma_key] += 1
            key, val, inc = dma_key, 16 * self.dmacnt[dma_key], 16
        elif own_sem is not None:
            key = own_sem
            sem = self._sem(key, dma=False)
            val, inc = 1, 1
        else:
            self.cnt[eng] += 1
            key, val, inc, sem = eng, self.cnt[eng], 1, self.sems[eng]
        self.q[eng].append((wl, fn, sem, inc))
        self.nops += 1
        for (name, p0, p1, f0, f1) in regs_w:
            lst = self.recs.setdefault(name, [])
            lst[:] = [r for r in lst if not (p0 <= r[0] and r[1] <= p1 and f0 <= r[2] and r[3] <= f1)]
            lst.append((p0, p1, f0, f1, key, val, True, eng))
        for (name, p0, p1, f0, f1) in regs_r:
            lst = self.recs.setdefault(name, [])
            lst[:] = [r for r in lst if not ((not r[6]) and r[4] == key and p0 <= r[0] and r[1] <= p1
                                             and f0 <= r[2] and r[3] <= f1)]
            lst.append((p0, p1, f0, f1, key, val, False, eng))

    def barrier(self):
        targets = []
        for e in ENGS:
            if self.cnt[e] > 0:
                targets.append((e, self.cnt[e]))
        for k, c in self.dmacnt.items():
            if c > 0:
                targets.append((k, 16 * c))
        for k in self.sems:
            if k not in ENGS and k not in self.dmacnt:
                targets.append((k, 1))
        for e in ENGS:
            wl = []
            for key, val in targets:
                if self.seen[e].get(key, 0) >= val:
                    continue
                self.seen[e][key] = val
                wl.append((self.sems[key], val))
            if wl:
                self.q[e].append((wl, None, None, 0))
        self.recs = {}

    def emit(self):
        nc = self.nc
        qs = self.q
        with nc.Block() as block:
            def run(engine, lst):
                for (wl, fn, sem, inc) in lst:
                    for (s, v) in wl:
                        engine.wait_ge(s, v)
                    if fn is not None:
                        fn(engine).then_inc(sem, inc)

            @block.tensor
            def _(e):
                run(e, qs["pe"])

            @block.scalar
            def _(e):
                run(e, qs["act"])

            @block.vector
            def _(e):
                run(e, qs["dve"])

            @block.gpsimd
            def _(e):
                run(e, qs["pool"])

            @block.sync
            def _(e):
                run(e, qs["sp"])

    def dma(self, out, in_, eng="sp", key=None):
        if key is None:
            if not type(out.tensor).__name__.startswith("DRam"):
                side = out
            elif not type(in_.tensor).__name__.startswith("DRam"):
                side = in_
            else:
                side = out
            name, p0, _, f0, _ = self.region(side)
            if not name.startswith("slot"):
                name = "".join(ch for ch in name if not ch.isdigit())
            key = f"{name}@{p0}_{f0}"
        self.op(eng, lambda e: e.dma_start(out=out, in_=in_), [in_], [out], dma_key=key)

    def mm(self, out, lhsT, rhs, start=True, stop=True):
        self.op("pe", lambda e: e.matmul(out, lhsT, rhs, start=start, stop=stop), [lhsT, rhs], [out])

    def tr(self, out, in_, ident):
        self.op("pe", lambda e: e.transpose(out, in_, ident), [in_, ident], [out])

    def act(self, out, in_, func, bias=None, scale=1.0, accum_out=None):
        reads = [in_]
        kw = {}
        if bias is not None:
            kw["bias"] = bias
            if not isinstance(bias, (int, float)):
                reads.append(bias)
        if not isinstance(scale, (int, float)):
            reads.append(scale)
        kw["scale"] = scale
        writes = [out]
        if accum_out is not None:
            kw["accum_out"] = accum_out
            writes.append(accum_out)
        self.op("act", lambda e: e.activation(out=out, in_=in_, func=func, **kw), reads, writes)

    def tt(self, out, in0, in1, op, eng="dve"):
        self.op(eng, lambda e: e.tensor_tensor(out=out, in0=in0, in1=in1, op=op), [in0, in1], [out])

    def ts(self, out, in0, s1, op0, s2=None, op1=None, eng="dve"):
        reads = [in0] + [s for s in (s1, s2) if s is not None and not isinstance(s, (int, float))]
        kw = {}
        if op1 is not None:
            kw["op1"] = op1
        self.op(eng, lambda e: e.tensor_scalar(out=out, in0=in0, scalar1=s1, scalar2=s2, op0=op0, **kw), reads, [out])

    def stt(self, out, in0, scalar, in1, op0, op1):
        reads = [in0, in1] + ([scalar] if not isinstance(scalar, (int, float)) else [])
        self.op("dve", lambda e: e.scalar_tensor_tensor(out=out, in0=in0, scalar=scalar, in1=in1, op0=op0, op1=op1),
                reads, [out])

    def copy(self, out, in_, eng="dve"):
        if eng == "act":
            self.op("act", lambda e: e.copy(out=out, in_=in_), [in_], [out])
        else:
            self.op(eng, lambda e: e.tensor_copy(out=out, in_=in_), [in_], [out])

    def memset(self, ap, val, eng="dve"):
        self.op(eng, lambda e: e.memset(ap, val), [], [ap])

    def scan(self, out, d0, d1, initial, op0, op1):
        reads = [d0, d1] + ([initial] if not isinstance(initial, (int, float)) else [])
        self.op("dve", lambda e: e.tensor_tensor_scan(out=out, data0=d0, data1=d1, initial=initial, op0=op0, op1=op1),
                reads, [out])

    def reduce(self, out, in_, op, axis=AX.X):
        self.op("dve", lambda e: e.tensor_reduce(out=out, in_=in_, axis=axis, op=op), [in_], [out])

    def bn_stats(self, out, in_):
        self.op("dve", lambda e: e.bn_stats(out=out, in_=in_), [in_], [out])

    def bn_aggr(self, out, in_):
        self.op("dve", lambda e: e.bn_aggr(out=out, in_=in_), [in_], [out])

    def collective(self, src, dst, name):
        self.op("pool", lambda e: e.collective_compute("AllGather", ALU.bypass,
                                                       replica_groups=[list(range(NCORES))],
                                                       ins=[src], outs=[dst]), [src], [dst], own_sem="cc_" + name)
        key = "cc_" + name
        self.seen["pool"][key] = 1
        self.q["pool"].append(([(self.sems[key], 1)], None, None, 0))

    def dump(self, name, ap, eng="sp"):
        d = self.nc.dram_tensor("dbg_" + name, list(ap.shape), ap.dtype, kind="ExternalOutput").ap()
        self.dma(d, ap, eng=eng)
        self.dumps.append("dbg_" + name)


U_GROUPS = (6, 6, 5, 5)


def weight_blocks():
    B = []
    for h in range(4):
        B.append((f"G1_{h}", "w_in", 0, D, [(512 + 128 * h, 128), (1024 + 256 * h, 256), (128 * h, 128)]))
        B.append((f"G2_{h}", "w_in", 0, D, [(2048 + 256 * h, 256)]))
    for g in range(2):
        B.append((f"S1_{g}", "w_in", 0, D, [(4112 + 512 * g, 512)]))
        B.append((f"S2_{g}", "w_in", 0, D, [(5136 + 128 * g, 128), (5392 + 128 * g, 128)]))
        B.append((f"S3_{g}", "w_in", 0, D, [(3088 + 512 * g, 512)]))
    for j in range(4):
        B.append((f"O_{j}", "w_out", 0, D, [(512 * j, 512)]))
    for j in range(4):
        B.append((f"K_{j}", "xa_wk", 0, D, [(512 * j, 512)]))
    for j in range(4):
        B.append((f"V_{j}", "xa_wv", 0, D, [(512 * j, 512)]))
    for j in range(4):
        B.append((f"Q_{j}", "xa_wq", 0, D, [(512 * j, 512)]))
    for j in range(4):
        B.append((f"WO_{j}", "xa_wo", 0, D, [(512 * j, 512)]))
    for i in range(22):
        B.append((f"U_{i}", "ffn_w_up", 0, D, [(256 * i, 256), (D_FF + 256 * i, 256)]))
    c0 = 0
    for gi, nu in enumerate(U_GROUPS):
        nk = 2 * nu
        for j in range(4):
            B.append((f"D_{gi}_{j}", "ffn_w_down", c0 * 128, nk * 128, [(512 * j, 512)]))
        c0 += nk
    return B


def block_table():
    tab = {}
    off = 0
    for (name, mat, r0, nr, cols) in weight_blocks():
        nk = nr // 128
        ncols = sum(c for _, c in cols)
        tab[name] = (off, nk, ncols)
        off += nk * ncols
    return tab, off


BTAB, WL = block_table()
PIECE = 262144
assert WL == 2 * PIECE
for _n, (_o, _k, _c) in BTAB.items():
    assert _o // PIECE == (_o + _k * _c - 1) // PIECE, _n

PC_NEGB, PC_CW, PC_CB, PC_FW, PC_FB = 0, 4, 52, 64, 196
PC_N = 240
PR_GNW, PR_SNW, PR_DTB, PR_ALOG, PR_SD = 0, 1024, 2048, 2064, 2080
PR_LN = 2096
PR_N = PR_LN + 6 * D
C_IDENT, C_CAUSAL, C_STRICT, C_ONES, C_RESET = 0, 128, 256, 384, 512
C_MISC = 512 + T
CN = C_MISC + 8


def host_weights(inp, l):
    out = np.zeros((128, WL), np.float32)
    for (name, mat, r0, nr, cols) in weight_blocks():
        off, nk, ncols = BTAB[name]
        W = inp[mat][l]
        sub = np.concatenate([W[r0:r0 + nr, c0:c0 + c] for c0, c in cols], axis=1)
        out[:, off:off + nk * ncols] = sub.reshape(nk, 128, ncols).transpose(1, 0, 2).reshape(128, nk * ncols)
    return out


def host_sblock(inp, l):
    W = inp["w_in"][l]
    sub = np.concatenate([W[:, 3072:3088], W[:, 5648:5664]], axis=1)
    return np.ascontiguousarray(sub.reshape(16, 128, 32).transpose(1, 0, 2).reshape(128, 512))


def host_small(inp, l):
    pc = np.zeros((128, PC_N), np.float32)
    pc[:, PC_NEGB:PC_NEGB + 4] = -inp["gla_b_gate"][l].reshape(4, 128).T
    cw = inp["ssd_conv_w"][l]
    pc[:, PC_CW:PC_CW + 48] = cw.reshape(4, 12, 128).transpose(2, 1, 0).reshape(128, 48)
    pc[:, PC_CB:PC_CB + 12] = inp["ssd_conv_b"][l].reshape(12, 128).T
    fw = inp["ffn_conv_w"][l]
    pc[:, PC_FW:PC_FW + 132] = fw.reshape(3, 44, 128).transpose(2, 1, 0).reshape(128, 132)
    pc[:, PC_FB:PC_FB + 44] = inp["ffn_conv_b"][l].reshape(44, 128).T
    pr = np.zeros((1, PR_N), np.float32)
    pr[0, PR_GNW:PR_GNW + 1024] = inp["gla_norm_w"][l]
    pr[0, PR_SNW:PR_SNW + 1024] = inp["ssd_norm_w"][l]
    pr[0, PR_DTB:PR_DTB + 16] = inp["ssd_dt_bias"][l]
    pr[0, PR_ALOG:PR_ALOG + 16] = inp["ssd_a_log"][l]
    pr[0, PR_SD:PR_SD + 16] = inp["ssd_d"][l]
    for i, k in enumerate(("ln_mix_g", "ln_mix_b", "ln_xa_g", "ln_xa_b", "ln_ffn_g", "ln_ffn_b")):
        pr[0, PR_LN + i * D:PR_LN + (i + 1) * D] = inp[k][l]
    return pc, pr, np.ascontiguousarray(inp["gla_w_gate"][l])


def host_consts():
    c = np.zeros((128, CN), np.float32)
    i = np.arange(128)
    c[:, C_IDENT:C_IDENT + 128] = np.eye(128)
    c[:, C_CAUSAL:C_CAUSAL + 128] = (i[:, None] <= i[None, :])
    c[:, C_STRICT:C_STRICT + 128] = (i[:, None] > i[None, :])
    c[:, C_ONES:C_ONES + 128] = 1.0
    r = np.ones(T, np.float32)
    r[::128] = 0.0
    c[:, C_RESET:C_RESET + T] = r[None, :]
    c[:, C_MISC + 0] = LN_EPS
    c[:, C_MISC + 1] = RMS_EPS
    c[:, C_MISC + 2] = 1.0
    c[:, C_MISC + 3] = np.log(128.0 ** -0.5)
    c[:, C_MISC + 4] = 0.0
    return c


class Builder:
    def __init__(self, n_layers, stop_after=None, cc=True, wl_dev=None, kind="F", wl_base=0):
        self.nl = n_layers
        self.stop_after = stop_after
        self.cc = cc
        self.kind = kind
        self.wl_base = wl_base
        self.wl_dev = WL if wl_dev is None else wl_dev
        nc = self.nc = bass.Bass("TRN2", target_bir_lowering=False)
        self.root = contextlib.ExitStack()
        P = self.P = Prog(nc, self.root)
        dt = nc.dram_tensor
        self.x_in = dt("x", [T, D], F32, kind="ExternalInput").ap()
        self.xh_in = dt("xhalo", [128, 16, 3], F32, kind="ExternalInput").ap()
        self.xT_in = dt("xT", [128, 16, T], F32, kind="ExternalInput").ap() if kind != "F" else None
        self.memT_in = dt("memT", [128, 16, NMEM], F32, kind="ExternalInput").ap()
        self.consts_in = dt("consts", [128, CN], F32, kind="ExternalInput").ap()
        self.cmask_in = dt("cmask", [128, 24], F32, kind="ExternalInput").ap()
        if cc:
            self.wsh = [dt(f"wsh{l}", [16, WL], F32, kind="ExternalInput").ap() for l in range(n_layers)]
        else:
            self.wsh = [dt(f"wsh{l}", [128, self.wl_dev], F32, kind="ExternalInput").ap() for l in range(n_layers)]
        self.ws_in = dt("wsblk", [n_layers, 128, 512], F32, kind="ExternalInput").ap()
        self.pcol_in = dt("pcol", [n_layers, 128, PC_N], F32, kind="ExternalInput").ap()
        self.prow_in = dt("prow", [n_layers, PR_N], F32, kind="ExternalInput").ap()
        self.wgate_in = dt("wgate", [n_layers, 16, 512], F32, kind="ExternalInput").ap()
        if kind == "A":
            self.outC = dt("outC", [128, 2048], F32, kind="ExternalOutput").ap()
            self.outD = dt("outD", [128, 2048], F32, kind="ExternalOutput").ap()
        else:
            self.out = dt("out", [T, D], F32, kind="ExternalOutput").ap()
        if kind == "B":
            self.gC_in = dt("gC", [1024, 2048], F32, kind="ExternalInput").ap()
            self.gD_in = dt("gD", [1024, 2048], F32, kind="ExternalInput").ap()
        self.XD = dt("XD", [T, D], F32, kind="Internal").ap()
        if cc:
            self.wb = [[dt(f"wb{l}_{k}", [512, 8192], BF16, kind="Internal").ap() for k in range(2)] for l in range(n_layers)]
            self.wg_t = [[dt(f"wg{l}_{k}", [4096, 8192], BF16, kind="Internal").ap() for k in range(2)] for l in range(n_layers)]
            self.wg = [[w.rearrange("(q a) b -> q (a b)", a=32) for w in ws_] for ws_ in self.wg_t]
        else:
            self.wg1 = None
        self.ncc = 0

    def sb(self, stack, name, shape, dtype):
        return stack.enter_context(self.nc.sbuf_tensor(name, shape, dtype))

    def bank(self):
        b = self.ps[self.pi % 8]
        self.pi += 1
        return b

    def exch(self, name, src_sb):
        nc = self.nc
        s = nc.dram_tensor(f"ex_s_{name}", [128, 2048], F32, kind="Internal").ap()
        d = nc.dram_tensor(f"ex_d_{name}", [1024, 2048], F32, kind="Internal").ap()
        self.P.dma(s, src_sb, eng="pool")
        if self.cc:
            self.P.collective(s, d, name)
        else:
            for r in range(NCORES):
                self.P.dma(d[r * 128:(r + 1) * 128, :], s, eng="pool")
        return d

    def wload(self, l, name):
        off, nk, ncols = BTAB[name]
        slot = self.slots[self.si % len(self.slots)]
        self.si += 1
        n = nk * ncols
        if self.cc:
            k = off // PIECE
            src = self.wg[l][k][:, off - k * PIECE:off - k * PIECE + n]
        else:
            src = self.wsh[l][:, off - self.wl_base:off - self.wl_base + n]
            self.P.dma(slot[:, 0:n], src, eng="pool")
            return slot[:, 0:n].rearrange("p (k c) -> p k c", k=nk)
        self.P.dma(slot[:, 0:n], src, eng="sp")
        return slot[:, 0:n].rearrange("p (k c) -> p k c", k=nk)

    def prefetch(self, l, names):
        views = {}
        order = list(names)
        depth = len(self.slots) - 1
        state = {"next": 0}

        def get(i):
            while state["next"] < len(order) and state["next"] <= i + depth - 1:
                j = state["next"]
                views[j] = self.wload(l, order[j])
                state["next"] += 1
            return views.pop(i)
        return get

    def build(self):
        nc, P = self.nc, self.P
        root = self.root
        self.ps = [root.enter_context(nc.psum_tensor(f"ps{i}", [128, 512], F32)) for i in range(8)]
        self.pi = 0
        self.XT = self.sb(root, "XT", [128, 16, T], BF16)
        self.XTh = self.sb(root, "XTh", [128, 16, 4], BF16)
        self.slots = [self.sb(root, f"slot{i}", [128, 8192], BF16) for i in range(3)]
        self.si = 0
        cst = self.cst = self.sb(root, "cst", [128, CN], F32)
        self.cm = self.sb(root, "cmask_sb", [128, 24], F32)
        self.identB = self.sb(root, "identB", [128, 128], BF16)
        self.causalB = self.sb(root, "causalB", [128, 128], BF16)
        P.dma(cst[:], self.consts_in)
        P.dma(self.cm[:], self.cmask_in)
        self.identF = cst[:, C_IDENT:C_IDENT + 128]
        self.causal = cst[:, C_CAUSAL:C_CAUSAL + 128]
        self.strict = cst[:, C_STRICT:C_STRICT + 128]
        self.ones = cst[:, C_ONES:C_ONES + 128]
        self.reset = cst[:, C_RESET:C_RESET + T]
        self.c_lneps = cst[:, C_MISC + 0:C_MISC + 1]
        self.c_rmseps = cst[:, C_MISC + 1:C_MISC + 2]
        self.c_one = cst[:, C_MISC + 2:C_MISC + 3]
        self.c_lnq = cst[:, C_MISC + 3:C_MISC + 4]
        P.copy(self.identB[:], self.identF)
        P.copy(self.causalB[:], self.causal)
        self.gather_weights(0)
        with contextlib.ExitStack() as st:
            xs = self.sb(st, "xs0", [128, 2, D], F32)
            hs = self.sb(st, "hs0", [128, 16, 3], F32)
            P.dma(hs[:], self.xh_in)
            P.copy(self.XTh[:, :, 0:3], hs[:])
            if self.xT_in is not None:
                for c4 in range(4):
                    P.dma(self.XT[:, 4 * c4:4 * c4 + 4, :], self.xT_in[:, 4 * c4:4 * c4 + 4, :], eng="pool")
            else:
                for t in range(NT):
                    P.dma(xs[:, t % 2, :], self.x_in[t * 128:(t + 1) * 128, :])
                    self.make_xT(t, xs[:, t % 2, :])
            P.barrier()
        if self.kind == "F":
            for l in range(self.nl):
                self.layer(l)
                if self.stop_after is not None and l == self.nl - 1:
                    break
        else:
            with contextlib.ExitStack() as lst:
                self.pc = self.sb(lst, "pc0", [128, PC_N], F32)
                P.dma(self.pc[:], self.pcol_in[0])
                if self.kind == "A":
                    self.mixer_A()
                elif self.kind == "B":
                    self.mixer_B()
                    self.xattn(0, last=True)
                else:
                    self.ffn(0, last=True, standalone=True)
                P.barrier()
        P.barrier()
        P.emit()
        self.root.close()
        return nc

    def gather_weights(self, l):
        P = self.P
        if self.cc:
            for k in range(2):
                P.dma(self.wb[l][k].rearrange("(p a) b -> p a b", a=32),
                      self.wsh[l][:, k * PIECE:(k + 1) * PIECE].rearrange("p (a b) -> p a b", a=32), eng="pool")
                P.collective(self.wb[l][k], self.wg_t[l][k], f"w{l}_{k}")
        else:
            pass

    def make_xT(self, t, xtile):
        P = self.P
        for cg in range(4):
            b = self.bank()
            for j in range(4):
                c = 4 * cg + j
                P.tr(b[:, j * 128:(j + 1) * 128], xtile[:, c * 128:(c + 1) * 128], self.identF)
            dst = self.XT[:, 4 * cg:4 * cg + 4, t * 128:(t + 1) * 128]
            src = b[:].rearrange("p (j c) -> p j c", j=4)
            if cg % 2 == 0:
                P.copy(dst, src, eng="act")
            else:
                P.copy(dst, src, eng="dve")

    def layer_norm_tile(self, st, t, xr_t, gb, bb, l, last):
        P = self.P
        stats = self.ln_stats
        for j in range(4):
            P.bn_stats(stats[:, j, :], xr_t[:, j * 512:(j + 1) * 512])
        P.bn_aggr(self.ln_mv[:, 0:2], stats[:].rearrange("p a b -> p (a b)"))
        P.act(self.ln_mv[:, 2:3], self.ln_mv[:, 1:2], AF.Ln, bias=self.c_lneps, scale=1.0)
        P.act(self.ln_mv[:, 3:4], self.ln_mv[:, 2:3], AF.Exp, scale=-0.5)
        P.ts(xr_t, xr_t, self.ln_mv[:, 0:1], ALU.subtract, self.ln_mv[:, 3:4], ALU.mult)
        P.tt(xr_t, xr_t, gb, ALU.mult)
        P.tt(xr_t, xr_t, bb, ALU.add)
        dst = self.out if last else self.XD
        P.dma(dst[t * 128:(t + 1) * 128, :], xr_t)
        self.make_xT(t, xr_t)

    def load_ln_params(self, st, l, idx):
        P = self.P
        gb = self.sb(st, f"lng{l}_{idx}", [128, D], F32)
        bb = self.sb(st, f"lnb{l}_{idx}", [128, D], F32)
        o = PR_LN + 2 * idx * D
        P.dma(gb[:], self.prow_in[l:l + 1, o:o + D].to_broadcast([128, D]))
        P.dma(bb[:], self.prow_in[l:l + 1, o + D:o + 2 * D].to_broadcast([128, D]))
        self.ln_stats = self.sb(st, f"lnst{l}_{idx}", [128, 4, 6], F32)
        self.ln_mv = self.sb(st, f"lnmv{l}_{idx}", [128, 4], F32)
        return gb, bb

    def proj_epilogue_phase(self, l, idx, srcT, wnames, last=False, xsrc=None):
        P = self.P
        if xsrc is None:
            xsrc = self.x_in if (l == 0 and idx == 0) else self.XD
        with contextlib.ExitStack() as st:
            XR = self.sb(st, f"XR{l}_{idx}", [128, NT, D], F32)
            xin = self.sb(st, f"xin{l}_{idx}", [128, 3, 512], F32)
            gb, bb = self.load_ln_params(st, l, idx)
            get = self.prefetch(l, wnames)
            k = 0
            for j in range(4):
                W = get(j)
                for t in range(NT):
                    xi = xin[:, k % 3, :]
                    k += 1
                    P.dma(xi, xsrc[t * 128:(t + 1) * 128, j * 512:(j + 1) * 512])
                    b = self.bank()
                    for kc in range(16):
                        P.mm(b[:], srcT[:, kc, t * 128:(t + 1) * 128], W[:, kc, :], start=(kc == 0), stop=(kc == 15))
                    P.stt(XR[:, t, j * 512:(j + 1) * 512], xi, ALPHA, b[:], ALU.mult, ALU.add)
            for t in range(NT):
                self.layer_norm_tile(st, t, XR[:, t, :], gb[:], bb[:], l, last)
            P.barrier()

    def layer(self, l):
        P = self.P
        with contextlib.ExitStack() as lst:
            self.pc = self.sb(lst, f"pc{l}", [128, PC_N], F32)
            P.dma(self.pc[:], self.pcol_in[l])
            self.mixer(l, lst)
            if self.stop_after in ("mixed", "mixer"):
                if self.stop_after == "mixer":
                    P.dma(self.out, self.XD, eng="sp")
                return
            self.xattn(l)
            if self.stop_after == "xattn":
                P.dma(self.out, self.XD, eng="sp")
                return
            if self.stop_after == "ffn":
                d2 = self.nc.dram_tensor("dbg_x2", [T, D], F32, kind="ExternalOutput").ap()
                P.dma(d2, self.XD, eng="sp")
            self.ffn(l, last=(l == self.nl - 1))
            P.barrier()

    def mixer(self, l, lst):
        P = self.P
        with contextlib.ExitStack() as ms:
            mixedT = self.sb(ms, f"mixedT{l}", [128, 16, T], BF16)
            with contextlib.ExitStack() as pst:
                packC = self.sb(pst, f"packC{l}", [128, 2048], F32)
                packD = self.sb(pst, f"packD{l}", [128, 2048], F32)
                Sg = packC[:, 0:1024].rearrange("p (h v) -> p h v", h=4)
                Ss = packC[:, 1024:2048].rearrange("p (g c) -> p g c", g=2)
                P.memset(packC[:], 0.0)
                P.memset(packD[:], 0.0)
                self.mixer_pass(l, ms, False, Sg, Ss, packD, None)
                P.barrier()
                dstC = self.exch(f"sc{l}", packC[:])
                dstD = self.exch(f"sd{l}", packD[:])
                with contextlib.ExitStack() as cs:
                    g2 = self.sb(cs, f"g2_{l}", [128, 2, 2048], F32)
                    gd = self.sb(cs, f"gd_{l}", [128, 8, 32], F32)
                    deff = self.sb(cs, f"deff{l}", [128, 20], F32)
                    P.dma(gd[:], dstD.rearrange("(r p) x -> p r x", p=128)[:, :, 0:32])
                    P.memset(packC[:], 0.0)
                    Sg_in = packC[:, 0:1024].rearrange("p (h v) -> p h v", h=4)
                    Ss_in = packC[:, 1024:2048].rearrange("p (h c) -> p h c", h=16)
                    for r in range(NCORES - 1):
                        gr = g2[:, r % 2, :]
                        P.dma(gr, dstC[r * 128:(r + 1) * 128, :])
                        m = self.cm[:, 8 + r:9 + r]
                        m1 = self.cm[:, 16 + r:17 + r]
                        P.ts(deff[:], gd[:, r, 0:20], m, ALU.mult, m1, ALU.add)
                        P.ts(gr, gr, m, ALU.mult)
                        for h in range(4):
                            P.stt(Sg_in[:, h, :], Sg_in[:, h, :], deff[:, h:h + 1],
                                  gr[:, 256 * h:256 * (h + 1)], ALU.mult, ALU.add)
                        P.tt(Ss_in, Ss_in, deff[:, 4:20].unsqueeze(2).to_broadcast([128, 16, 64]), ALU.mult)
                        P.tt(packC[:, 1024:2048], packC[:, 1024:2048], gr[:, 1024:2048], ALU.add)
                    P.barrier()
                Sg2, Ss2 = Sg, Ss
                self.mixer_pass(l, ms, True, Sg2, Ss2, None, mixedT)
                P.barrier()
            if self.stop_after == "mixed":
                P.dump("mixedT", mixedT[:])
                return
            self.proj_epilogue_phase(l, 0, mixedT, [f"O_{j}" for j in range(4)])

    def mixer_A(self):
        P = self.P
        with contextlib.ExitStack() as pst:
            packC = self.sb(pst, "packC0", [128, 2048], F32)
            packD = self.sb(pst, "packD0", [128, 2048], F32)
            Sg = packC[:, 0:1024].rearrange("p (h v) -> p h v", h=4)
            Ss = packC[:, 1024:2048].rearrange("p (g c) -> p g c", g=2)
            P.memset(packC[:], 0.0)
            P.memset(packD[:], 0.0)
            self.mixer_pass(0, pst, False, Sg, Ss, packD, None)
            P.dma(self.outC, packC[:])
            P.dma(self.outD, packD[:])
            P.barrier()

    def mixer_B(self):
        P = self.P
        l = 0
        with contextlib.ExitStack() as ms:
            mixedT = self.sb(ms, "mixedT0", [128, 16, T], BF16)
            with contextlib.ExitStack() as pst:
                packC = self.sb(pst, "packC0", [128, 2048], F32)
                Sg = packC[:, 0:1024].rearrange("p (h v) -> p h v", h=4)
                Ss = packC[:, 1024:2048].rearrange("p (g c) -> p g c", g=2)
                with contextlib.ExitStack() as cs:
                    g2 = self.sb(cs, "g2_0", [128, 2, 2048], F32)
                    gd = self.sb(cs, "gd_0", [128, 8, 32], F32)
                    deff = self.sb(cs, "deff0", [128, 20], F32)
                    P.dma(gd[:], self.gD_in.rearrange("(r p) x -> p r x", p=128)[:, :, 0:32])
                    P.memset(packC[:], 0.0)
                    Ss_in = packC[:, 1024:2048].rearrange("p (h c) -> p h c", h=16)
                    for r in range(NCORES - 1):
                        gr = g2[:, r % 2, :]
                        P.dma(gr, self.gC_in[r * 128:(r + 1) * 128, :])
                        m = self.cm[:, 8 + r:9 + r]
                        m1 = self.cm[:, 16 + r:17 + r]
                        P.ts(deff[:], gd[:, r, 0:20], m, ALU.mult, m1, ALU.add)
                        P.ts(gr, gr, m, ALU.mult)
                        for h in range(4):
                            P.stt(Sg[:, h, :], Sg[:, h, :], deff[:, h:h + 1],
                                  gr[:, 256 * h:256 * (h + 1)], ALU.mult, ALU.add)
                        P.tt(Ss_in, Ss_in, deff[:, 4:20].unsqueeze(2).to_broadcast([128, 16, 64]), ALU.mult)
                        P.tt(packC[:, 1024:2048], packC[:, 1024:2048], gr[:, 1024:2048], ALU.add)
                    P.barrier()
                self.mixer_pass(l, ms, True, Sg, Ss, None, mixedT)
                P.barrier()
            self.proj_epilogue_phase(l, 0, mixedT, [f"O_{j}" for j in range(4)], xsrc=self.x_in)

    def mixer_pass(self, l, ms, passB, Sg, Ss, pack, mixedT):
        P = self.P
        XT = self.XT
        tg = "B" if passB else "A"
        with contextlib.ExitStack() as st:
            names = []
            for h in range(4):
                names.append(f"G1_{h}")
                if passB:
                    names.append(f"G2_{h}")
            for g in range(2):
                names += [f"S1_{g}", f"S2_{g}"]
                if passB:
                    names.append(f"S3_{g}")
            get = self.prefetch(l, names)
            wi = 0
            wsb = self.sb(st, f"wsb{l}{tg}", [128, 512], BF16)
            P.dma(wsb[:], self.ws_in[l], eng="pool")
            WS = wsb[:].rearrange("p (k c) -> p k c", k=16)
            alrT = self.sb(st, f"alrT{l}{tg}", [16, T], F32)
            wgate = self.sb(st, f"wgate{l}{tg}", [16, 512], F32)
            P.dma(wgate[:], self.wgate_in[l])
            for half in range(2):
                b = self.bank()
                for kc in range(16):
                    P.mm(b[0:16, :], WS[:, kc, 0:16], XT[:, kc, half * 512:(half + 1) * 512], start=(kc == 0), stop=(kc == 15))
                P.copy(alrT[:, half * 512:(half + 1) * 512], b[0:16, :], eng="act")
            dt_ = self.sb(st, f"dt{l}{tg}", [128, NT, 16], F32)
            dtA = self.sb(st, f"dtA{l}{tg}", [128, NT, 16], F32)
            rows = self.sb(st, f"rows{l}{tg}", [128, 48], F32)
            P.dma(rows[:], self.prow_in[l:l + 1, PR_DTB:PR_DTB + 48].to_broadcast([128, 48]))
            P.act(rows[:, 16:32], rows[:, 16:32], AF.Exp)
            P.ts(rows[:, 16:32], rows[:, 16:32], -1.0, ALU.mult)
            b = self.bank()
            for t in range(NT):
                for kc in range(16):
                    P.mm(b[:, t * 16:(t + 1) * 16], XT[:, kc, t * 128:(t + 1) * 128], WS[:, kc, 16:32], start=(kc == 0), stop=(kc == 15))
            b3 = b[:, 0:NT * 16].rearrange("p (t h) -> p t h", t=NT)
            P.tt(dt_[:], b3, rows[:, 0:16].unsqueeze(1).to_broadcast([128, NT, 16]), ALU.add)
            P.act(dt_[:], dt_[:], AF.Exp)
            P.act(dt_[:], dt_[:], AF.Ln, bias=self.c_one, scale=1.0)
            P.tt(dtA[:], dt_[:], rows[:, 16:32].unsqueeze(1).to_broadcast([128, NT, 16]), ALU.mult)

            with contextlib.ExitStack() as gs:
                f1 = self.sb(gs, f"gf1{l}{tg}", [128, T], F32)
                cs = self.sb(gs, f"gcs{l}{tg}", [128, T], F32)
                eend = self.sb(gs, f"geend{l}{tg}", [128, T], F32)
                kendT = self.sb(gs, f"kendT{l}{tg}", [128, T], BF16)
                kend_tm = self.sb(gs, f"kendtm{l}{tg}", [128, NT, 128], BF16)
                v_tm = self.sb(gs, f"vtm{l}{tg}", [128, NT, 256], BF16)
                dec = self.sb(gs, f"gdec{l}{tg}", [128, NT + 1], F32)
                if passB:
                    eb = self.sb(gs, f"geb{l}", [128, T], F32)
                    einv = self.sb(gs, f"geinv{l}", [128, T], F32)
                    kinvT = self.sb(gs, f"kinvT{l}", [128, T], BF16)
                    qdecT = self.sb(gs, f"qdecT{l}", [128, T], BF16)
                    sog = self.sb(gs, f"sog{l}", [128, NT, 256], BF16)
                    mixh = self.sb(gs, f"mixh{l}", [128, 2, 256], BF16)
                    Sbf = self.sb(gs, f"gSbf{l}", [128, 256], BF16)
                    sTm = self.sb(gs, f"sTm{l}", [128, 2, 128], BF16)
                    on = self.sb(gs, f"gon{l}", [128, 2, 256], F32)
                    junk = self.sb(gs, f"gjunk{l}", [128, 256], F32)
                    ssq = self.sb(gs, f"gssq{l}", [128, 2, 4], F32)
                    gnw = self.sb(gs, f"gnw{l}", [128, 1024], F32)
                    P.dma(gnw[:], self.prow_in[l:l + 1, PR_GNW:PR_GNW + 1024].to_broadcast([128, 1024]))
                cs3 = cs[:].rearrange("p (t c) -> p t c", c=128)
                for h in range(4):
                    W1 = get(wi); wi += 1
                    for half in range(2):
                        sl = slice(half * 512, (half + 1) * 512)
                        b = self.bank()
                        P.mm(b[:], wgate[:, h * 128:(h + 1) * 128], alrT[:, sl])
                        P.act(f1[:, sl], b[:], AF.Exp, bias=self.pc[:, PC_NEGB + h:PC_NEGB + h + 1], scale=-1.0)
                    P.act(f1[:], f1[:], AF.Ln, bias=self.c_one, scale=1.0)
                    P.scan(cs[:], self.reset, f1[:], 0.0, ALU.mult, ALU.add)
                    P.act(dec[:, 0:NT], cs3[:, :, 127], AF.Exp, scale=-1.0 / 16.0)
                    P.tt(f1[:].rearrange("p (t c) -> p t c", c=128), cs3, cs3[:, :, 127:128].to_broadcast([128, NT, 128]), ALU.subtract)
                    P.act(eend[:], f1[:], AF.Exp, scale=1.0 / 16.0)
                    if not passB:
                        P.reduce(dec[:, NT:NT + 1], cs3[:, :, 127], ALU.add)
                        P.act(pack[:, h:h + 1], dec[:, NT:NT + 1], AF.Exp, scale=-1.0 / 16.0)
                    else:
                        P.act(eb[:], cs[:], AF.Exp, bias=self.c_lnq, scale=-1.0 / 16.0)
                        P.act(einv[:], cs[:], AF.Exp, scale=1.0 / 16.0)
                    for half in range(2):
                        sl = slice(half * 512, (half + 1) * 512)
                        b = self.bank()
                        for kc in range(16):
                            P.mm(b[:], W1[:, kc, 0:128], XT[:, kc, sl], start=(kc == 0), stop=(kc == 15))
                        P.tt(kendT[:, sl], b[:], eend[:, sl], ALU.mult)
                        if passB:
                            P.tt(kinvT[:, sl], b[:], einv[:, sl], ALU.mult)
                            b2 = self.bank()
                            for kc in range(16):
                                P.mm(b2[:], W1[:, kc, 384:512], XT[:, kc, sl], start=(kc == 0), stop=(kc == 15))
                            P.tt(qdecT[:, sl], b2[:], eb[:, sl], ALU.mult)
                    for tq in range(2):
                        b = self.bank()
                        for j in range(4):
                            t = 4 * tq + j
                            P.mm(b[:, j * 128:(j + 1) * 128], kendT[:, t * 128:(t + 1) * 128], self.identB[:])
                        P.copy(kend_tm[:, 4 * tq:4 * tq + 4, :], b[:].rearrange("p (j c) -> p j c", j=4), eng="act")
                    for t2 in range(NT // 2):
                        b = self.bank()
                        for j in range(2):
                            t = 2 * t2 + j
                            for kc in range(16):
                                P.mm(b[:, j * 256:(j + 1) * 256], XT[:, kc, t * 128:(t + 1) * 128], W1[:, kc, 128:384], start=(kc == 0), stop=(kc == 15))
                        P.copy(v_tm[:, 2 * t2:2 * t2 + 2, :], b[:].rearrange("p (j c) -> p j c", j=2), eng=("act" if t2 % 2 else "dve"))
                    if passB:
                        W2 = get(wi); wi += 1
                        for t2 in range(NT // 2):
                            b = self.bank()
                            for j in range(2):
                                t = 2 * t2 + j
                                for kc in range(16):
                                    P.mm(b[:, j * 256:(j + 1) * 256], XT[:, kc, t * 128:(t + 1) * 128], W2[:, kc, :], start=(kc == 0), stop=(kc == 15))
                            P.act(sog[:, 2 * t2:2 * t2 + 2, :], b[:].rearrange("p (j c) -> p j c", j=2), AF.Silu)
                    for t in range(NT):
                        tsl = slice(t * 128, (t + 1) * 128)
                        if passB:
                            P.copy(Sbf[:], Sg[:, h, :], eng="act")
                            b = self.bank()
                            P.mm(b[:, 0:128], kinvT[:, tsl], qdecT[:, tsl])
                            P.tt(sTm[:, t % 2, :], b[:, 0:128], self.causal, ALU.mult)
                            ob = self.bank()
                            P.mm(ob[:, 0:256], sTm[:, t % 2, :], v_tm[:, t, :], start=True, stop=False)
                            P.mm(ob[:, 0:256], qdecT[:, tsl], Sbf[:], start=False, stop=True)
                            sq = ssq[:, t % 2, :]
                            P.act(junk[:], ob[:, 0:256], AF.Square, accum_out=sq[:, 0:1])
                            P.act(sq[:, 1:2], sq[:, 0:1], AF.Ln, bias=self.c_rmseps, scale=1.0 / 256.0)
                            P.act(sq[:, 2:3], sq[:, 1:2], AF.Exp, scale=-0.5)
                            P.stt(on[:, t % 2, :], ob[:, 0:256], sq[:, 2:3], gnw[:, h * 256:(h + 1) * 256], ALU.mult, ALU.mult)
                            P.tt(mixh[:, t % 2, :], on[:, t % 2, :], sog[:, t, :], ALU.mult)
                            self.to_mixedT_tile(mixh[:, t % 2, :], mixedT, 2 * h, 2, t)
                        cb = self.bank()
                        P.mm(cb[:, 0:256], kend_tm[:, t, :], v_tm[:, t, :])
                        P.stt(Sg[:, h, :], Sg[:, h, :], dec[:, t:t + 1], cb[:, 0:256], ALU.mult, ALU.add)
                P.barrier()

            with contextlib.ExitStack() as ss:
                pre = self.sb(ss, f"pre{l}{tg}", [128, 2, T + 4], BF16)
                cvT = self.sb(ss, f"cvT{l}{tg}", [128, 6, T], BF16)
                xs_tm = self.sb(ss, f"xstm{l}{tg}", [128, NT, 512], BF16)
                B_tm = self.sb(ss, f"Btm{l}{tg}", [128, NT, 128], BF16)
                dg = self.sb(ss, f"dg{l}{tg}", [128, 2, 4, 128], BF16)
                xdt = self.sb(ss, f"xdt{l}{tg}", [128, 2, 512], BF16)
                xend = self.sb(ss, f"xend{l}{tg}", [128, 2, 512], BF16)
                sm = self.sb(ss, f"ssm{l}{tg}", [128, 2, 64], F32)
                dtot = self.sb(ss, f"dtot{l}{tg}", [128, 16], F32)
                P.memset(dtot[:], 0.0)
                if passB:
                    sz = self.sb(ss, f"sz{l}", [128, NT, 512], BF16)
                    mixg = self.sb(ss, f"mixg{l}", [128, 2, 512], BF16)
                    Sbf = self.sb(ss, f"sSbf{l}", [128, 512], BF16)
                    cbm = self.sb(ss, f"cbm{l}", [128, 2, 128], F32)
                    tri = self.sb(ss, f"tri{l}", [128, 8, 128], F32)
                    Lm = self.sb(ss, f"Lm{l}", [128, 8, 128], BF16)
                    Wm = self.sb(ss, f"Wm{l}", [128, 8, 128], BF16)
                    ytmp = self.sb(ss, f"ytmp{l}", [128, 1, 512], F32)
                    y2 = self.sb(ss, f"y2{l}", [128, 1, 512], F32)
                    junk = self.sb(ss, f"sjunk{l}", [128, 512], F32)
                    ssq = self.sb(ss, f"sssq{l}", [128, 2, 4], F32)
                    snw = self.sb(ss, f"snw{l}", [128, 1024], F32)
                    P.dma(snw[:], self.prow_in[l:l + 1, PR_SNW:PR_SNW + 1024].to_broadcast([128, 1024]))
                for g in range(2):
                    W1 = get(wi); wi += 1
                    W2 = get(wi); wi += 1
                    nch = 6 if passB else 5
                    for ci in range(nch):
                        Wc = W1[:, :, ci * 128:(ci + 1) * 128] if ci < 4 else W2[:, :, (ci - 4) * 128:(ci - 3) * 128]
                        chan = (4 * g + ci) if ci < 4 else (8 + g if ci == 4 else 10 + g)
                        b = self.bank()
                        for kc in range(16):
                            P.mm(b[:, 0:3], Wc[:, kc, :], self.XTh[:, kc, 0:3], start=(kc == 0), stop=(kc == 15))
                        P.copy(pre[:, ci % 2, 0:3], b[:, 0:3], eng="act")
                        for half in range(2):
                            b = self.bank()
                            for kc in range(16):
                                P.mm(b[:], Wc[:, kc, :], XT[:, kc, half * 512:(half + 1) * 512], start=(kc == 0), stop=(kc == 15))
                            P.copy(pre[:, ci % 2, 3 + half * 512:3 + (half + 1) * 512], b[:], eng=("act" if half else "dve"))
                        dgi = dg[:, ci % 2, :, :]
                        for j in range(4):
                            P.ts(dgi[:, j, :], self.identF, self.pc[:, PC_CW + chan * 4 + j:PC_CW + chan * 4 + j + 1], ALU.mult)
                        for half in range(2):
                            b = self.bank()
                            for j in range(4):
                                P.mm(b[:], dgi[:, j, :], pre[:, ci % 2, half * 512 + j:half * 512 + j + 512], start=(j == 0), stop=(j == 3))
                            P.act(cvT[:, ci, half * 512:(half + 1) * 512], b[:], AF.Silu, bias=self.pc[:, PC_CB + chan:PC_CB + chan + 1], scale=1.0)
                    for t in range(NT):
                        b = self.bank()
                        for ci in range(4):
                            P.mm(b[:, ci * 128:(ci + 1) * 128], cvT[:, ci, t * 128:(t + 1) * 128], self.identB[:])
                        P.copy(xs_tm[:, t, :], b[:], eng=("act" if t % 2 else "dve"))
                    for tq in range(2):
                        b = self.bank()
                        for j in range(4):
                            t = 4 * tq + j
                            P.mm(b[:, j * 128:(j + 1) * 128], cvT[:, 4, t * 128:(t + 1) * 128], self.identB[:])
                        P.copy(B_tm[:, 4 * tq:4 * tq + 4, :], b[:].rearrange("p (j c) -> p j c", j=4), eng="act")
                    if passB:
                        W3 = get(wi); wi += 1
                        for t in range(NT):
                            b = self.bank()
                            for kc in range(16):
                                P.mm(b[:], XT[:, kc, t * 128:(t + 1) * 128], W3[:, kc, :], start=(kc == 0), stop=(kc == 15))
                            P.act(sz[:, t, :], b[:], AF.Silu)
                    hs = slice(8 * g, 8 * g + 8)
                    for t in range(NT):
                        tsl = slice(t * 128, (t + 1) * 128)
                        k2 = t % 2
                        b = self.bank()
                        P.mm(b[:, 0:8], self.causal, dtA[:, t, hs])
                        P.mm(b[:, 8:16], self.ones, dtA[:, t, hs])
                        P.copy(sm[:, k2, 0:16], b[:, 0:16], eng="act")
                        P.tt(sm[:, k2, 32:40], sm[:, k2, 8:16], sm[:, k2, 0:8], ALU.subtract)
                        P.act(sm[:, k2, 16:24], sm[:, k2, 0:8], AF.Exp)
                        P.act(sm[:, k2, 32:40], sm[:, k2, 32:40], AF.Exp)
                        P.act(sm[:, k2, 48:56], sm[:, k2, 8:16], AF.Exp)
                        if not passB:
                            P.tt(dtot[:, hs], dtot[:, hs], sm[:, k2, 8:16], ALU.add)
                        xs3 = xs_tm[:, t, :].rearrange("p (h c) -> p h c", h=8)
                        xdt3 = xdt[:, k2, :].rearrange("p (h c) -> p h c", h=8)
                        P.tt(xdt3, xs3, dt_[:, t, hs].unsqueeze(2).to_broadcast([128, 8, 64]), ALU.mult)
                        P.tt(xend[:, k2, :].rearrange("p (h c) -> p h c", h=8), xdt3,
                             sm[:, k2, 32:40].unsqueeze(2).to_broadcast([128, 8, 64]), ALU.mult)
                        if passB:
                            P.copy(Sbf[:], Ss[:, g, :], eng="act")
                            b = self.bank()
                            P.mm(b[:, 0:128], cvT[:, 4, tsl], cvT[:, 5, tsl])
                            P.tt(cbm[:, k2, :], b[:, 0:128], self.causal, ALU.mult)
                            P.tt(tri[:], self.causal.unsqueeze(1).to_broadcast([128, 8, 128]),
                                 dtA[:, t, hs].unsqueeze(2).to_broadcast([128, 8, 128]), ALU.mult)
                            for hq in range(2):
                                sb_ = self.bank()
                                for j in range(4):
                                    P.mm(sb_[:, j * 128:(j + 1) * 128], self.strict, tri[:, 4 * hq + j, :])
                                P.act(Lm[:, 4 * hq:4 * hq + 4, :], sb_[:].rearrange("p (j c) -> p j c", j=4), AF.Exp)
                            P.tt(Wm[:], Lm[:], cbm[:, k2, :].unsqueeze(1).to_broadcast([128, 8, 128]), ALU.mult)
                            yb = self.bank()
                            for hh in range(8):
                                P.mm(yb[:, hh * 64:(hh + 1) * 64], Wm[:, hh, :], xdt[:, k2, hh * 64:(hh + 1) * 64])
                            ob = self.bank()
                            P.mm(ob[:], cvT[:, 5, tsl], Sbf[:])
                            yt3 = ytmp[:, 0, :].rearrange("p (h c) -> p h c", h=8)
                            P.tt(yt3, ob[:].rearrange("p (h c) -> p h c", h=8),
                                 sm[:, k2, 16:24].unsqueeze(2).to_broadcast([128, 8, 64]), ALU.mult)
                            P.tt(ytmp[:, 0, :], ytmp[:, 0, :], yb[:], ALU.add)
                            y23 = y2[:, 0, :].rearrange("p (h c) -> p h c", h=8)
                            P.tt(y23, xs3, rows[:, 32 + 8 * g:40 + 8 * g].unsqueeze(2).to_broadcast([128, 8, 64]), ALU.mult)
                            P.tt(y2[:, 0, :], y2[:, 0, :], ytmp[:, 0, :], ALU.add)
                            P.tt(y2[:, 0, :], y2[:, 0, :], sz[:, t, :], ALU.mult)
                            sq = ssq[:, k2, :]
                            P.act(junk[:], y2[:, 0, :], AF.Square, accum_out=sq[:, 0:1])
                            P.act(sq[:, 1:2], sq[:, 0:1], AF.Ln, bias=self.c_rmseps, scale=1.0 / 512.0)
                            P.act(sq[:, 2:3], sq[:, 1:2], AF.Exp, scale=-0.5)
                            P.stt(mixg[:, k2, :], y2[:, 0, :], sq[:, 2:3], snw[:, g * 512:(g + 1) * 512], ALU.mult, ALU.mult)
                            self.to_mixedT_tile(mixg[:, k2, :], mixedT, 8 + 4 * g, 4, t)
                        cb = self.bank()
                        P.mm(cb[:], B_tm[:, t, :], xend[:, k2, :])
                        S3 = Ss[:, g, :].rearrange("p (h c) -> p h c", h=8)
                        P.tt(S3, S3, sm[:, k2, 48:56].unsqueeze(2).to_broadcast([128, 8, 64]), ALU.mult)
                        P.tt(Ss[:, g, :], Ss[:, g, :], cb[:], ALU.add)
                if not passB:
                    P.act(pack[:, 4:20], dtot[:], AF.Exp)
                P.barrier()

    def to_mixedT_tile(self, src, mixedT, c0, nch, t):
        P = self.P
        b = self.bank()
        for ci in range(nch):
            P.mm(b[:, ci * 128:(ci + 1) * 128], src[:, ci * 128:(ci + 1) * 128], self.identB[:])
        P.copy(mixedT[:, c0:c0 + nch, t * 128:(t + 1) * 128], b[:, 0:nch * 128].rearrange("p (j c) -> p j c", j=nch),
               eng=("act" if t % 2 else "dve"))

    def to_mixedT(self, src_tm, mixedT, c0, nch):
        P = self.P
        for ci in range(nch):
            for tq in range(2):
                b = self.bank()
                for j in range(4):
                    t = 4 * tq + j
                    P.mm(b[:, j * 128:(j + 1) * 128], src_tm[:, t, ci * 128:(ci + 1) * 128], self.identB[:])
                P.copy(mixedT[:, c0 + ci, 4 * tq * 128:(4 * tq + 4) * 128], b[:], eng=("act" if tq else "dve"))

    def xattn(self, l, last=False):
        P = self.P
        XT = self.XT
        with contextlib.ExitStack() as xs:
            oT = self.sb(xs, f"oT{l}", [128, 16, T], BF16)
            with contextlib.ExitStack() as st:
                memF = self.sb(st, f"memF{l}", [128, 16, NMEM], F32)
                memT = self.sb(st, f"memTb{l}", [128, 16, NMEM], BF16)
                kT = self.sb(st, f"kT{l}", [128, 16, NMEM], BF16)
                vx = self.sb(st, f"vx{l}", [128, 2, D], BF16)
                qT = self.sb(st, f"qT{l}", [128, 4, T], BF16)
                sc = self.sb(st, f"sc{l}", [128, 2, 4], F32)
                pe_ = self.sb(st, f"pexp{l}", [128, 2, NMEM], F32)
                pn = self.sb(st, f"pn{l}", [128, 2, NMEM], BF16)
                pT = self.sb(st, f"pT{l}", [128, 2, T], BF16)
                P.dma(memF[:], self.memT_in)
                P.copy(memT[:, 0:8, :], memF[:, 0:8, :], eng="act")
                P.copy(memT[:, 8:16, :], memF[:, 8:16, :], eng="dve")
                names = [f"K_{j}" for j in range(4)] + [f"V_{j}" for j in range(4)] + [f"Q_{j}" for j in range(4)]
                get = self.prefetch(l, names)
                for j in range(4):
                    W = get(j)
                    for cc in range(4):
                        b = self.bank()
                        for kc in range(16):
                            P.mm(b[:, 0:NMEM], W[:, kc, cc * 128:(cc + 1) * 128], memT[:, kc, :], start=(kc == 0), stop=(kc == 15))
                        P.copy(kT[:, 4 * j + cc, :], b[:, 0:NMEM], eng=("act" if cc % 2 else "dve"))
                for j in range(4):
                    W = get(4 + j)
                    for mt in range(2):
                        b = self.bank()
                        for kc in range(16):
                            P.mm(b[:], memT[:, kc, mt * 128:(mt + 1) * 128], W[:, kc, :], start=(kc == 0), stop=(kc == 15))
                        P.copy(vx[:, mt, j * 512:(j + 1) * 512], b[:], eng=("act" if mt else "dve"))
                scale = 512.0 ** -0.5
                for h in range(4):
                    W = get(8 + h)
                    for cc in range(4):
                        for half in range(2):
                            b = self.bank()
                            for kc in range(16):
                                P.mm(b[:], W[:, kc, cc * 128:(cc + 1) * 128], XT[:, kc, half * 512:(half + 1) * 512], start=(kc == 0), stop=(kc == 15))
                            P.copy(qT[:, cc, half * 512:(half + 1) * 512], b[:], eng=("act" if (cc + half) % 2 else "dve"))
                    for t in range(NT):
                        tsl = slice(t * 128, (t + 1) * 128)
                        k2 = t % 2
                        b = self.bank()
                        for cc in range(4):
                            P.mm(b[:, 0:NMEM], qT[:, cc, tsl], kT[:, 4 * h + cc, :], start=(cc == 0), stop=(cc == 3))
                        P.reduce(sc[:, k2, 0:1], b[:, 0:NMEM], ALU.max)
                        P.ts(sc[:, k2, 1:2], sc[:, k2, 0:1], -scale, ALU.mult)
                        P.act(pe_[:, k2, :], b[:, 0:NMEM], AF.Exp, bias=sc[:, k2, 1:2], scale=scale, accum_out=sc[:, k2, 2:3])
                        P.op("dve", lambda e, o=sc[:, k2, 3:4], i=sc[:, k2, 2:3]: e.reciprocal(out=o, in_=i), [sc[:, k2, 2:3]], [sc[:, k2, 3:4]])
                        P.ts(pn[:, k2, :], pe_[:, k2, :], sc[:, k2, 3:4], ALU.mult)
                        b2 = self.bank()
                        for mt in range(2):
                            P.mm(b2[:, mt * 128:(mt + 1) * 128], pn[:, k2, mt * 128:(mt + 1) * 128], self.identB[:])
                        P.copy(pT[:, :, tsl], b2[:, 0:256].rearrange("p (m c) -> p m c", m=2), eng="act")
                    for cc in range(4):
                        for half in range(2):
                            b = self.bank()
                            for mt in range(2):
                                P.mm(b[:], vx[:, mt, (4 * h + cc) * 128:(4 * h + cc + 1) * 128], pT[:, mt, half * 512:(half + 1) * 512], start=(mt == 0), stop=(mt == 1))
                            P.copy(oT[:, 4 * h + cc, half * 512:(half + 1) * 512], b[:], eng=("act" if (cc + half) % 2 else "dve"))
                P.barrier()
            self.proj_epilogue_phase(l, 1, oT, [f"WO_{j}" for j in range(4)], last=last)

    def ffn(self, l, last, standalone=False):
        P = self.P
        XT = self.XT
        if not standalone:
            self.halo_exchange(l, "f", 2)
            if l + 1 < self.nl:
                self.gather_weights(l + 1)
        with contextlib.ExitStack() as fs:
            XR = self.sb(fs, f"XRf{l}", [128, NT, D], F32)
            for t in range(NT):
                P.dma(XR[:, t, :], (self.x_in if standalone else self.XD)[t * 128:(t + 1) * 128, :])
            with contextlib.ExitStack() as st:
                hT = self.sb(st, f"hT{l}", [128, 12, T], BF16)
                pre = self.sb(st, f"fpre{l}", [128, 2, T + 4], BF16)
                dg = self.sb(st, f"fdg{l}", [128, 2, 3, 128], BF16)
                gl = self.sb(st, f"fgl{l}", [128, 2, T], F32)
                names = []
                ui = 0
                for gi, nu in enumerate(U_GROUPS):
                    names += [f"U_{ui + i}" for i in range(nu)]
                    names += [f"D_{gi}_{j}" for j in range(4)]
                    ui += nu
                get = self.prefetch(l, names)
                wi = 0
                ui = 0
                first = True
                for gi, nu in enumerate(U_GROUPS):
                    nk = 2 * nu
                    for i in range(nu):
                        W = get(wi); wi += 1
                        for c2 in range(2):
                            chan = 2 * (ui + i) + c2
                            kloc = 2 * i + c2
                            pr = pre[:, c2, :]
                            Wg_ = W[:, :, c2 * 128:(c2 + 1) * 128]
                            Wv_ = W[:, :, 256 + c2 * 128:256 + (c2 + 1) * 128]
                            b = self.bank()
                            for kc in range(16):
                                P.mm(b[:, 0:2], Wg_[:, kc, :], self.XTh[:, kc, 0:2], start=(kc == 0), stop=(kc == 15))
                            P.copy(pr[:, 0:2], b[:, 0:2], eng="act")
                            for half in range(2):
                                b = self.bank()
                                for kc in range(16):
                                    P.mm(b[:], Wg_[:, kc, :], XT[:, kc, half * 512:(half + 1) * 512], start=(kc == 0), stop=(kc == 15))
                                P.copy(pr[:, 2 + half * 512:2 + (half + 1) * 512], b[:], eng=("act" if half else "dve"))
                            for j in range(3):
                                P.ts(dg[:, c2, j, :], self.identF, self.pc[:, PC_FW + chan * 3 + j:PC_FW + chan * 3 + j + 1], ALU.mult)
                            for half in range(2):
                                b = self.bank()
                                for j in range(3):
                                    P.mm(b[:], dg[:, c2, j, :], pr[:, half * 512 + j:half * 512 + j + 512], start=(j == 0), stop=(j == 2))
                                P.act(gl[:, c2, half * 512:(half + 1) * 512], b[:], AF.Gelu, bias=self.pc[:, PC_FB + chan:PC_FB + chan + 1], scale=1.0)
                            for half in range(2):
                                b = self.bank()
                                for kc in range(16):
                                    P.mm(b[:], Wv_[:, kc, :], XT[:, kc, half * 512:(half + 1) * 512], start=(kc == 0), stop=(kc == 15))
                                P.tt(hT[:, kloc, half * 512:(half + 1) * 512], b[:], gl[:, c2, half * 512:(half + 1) * 512], ALU.mult)
                    for j in range(4):
                        Wd = get(wi); wi += 1
                        for t in range(NT):
                            b = self.bank()
                            for kc in range(nk):
                                P.mm(b[:], hT[:, kc, t * 128:(t + 1) * 128], Wd[:, kc, :], start=(kc == 0), stop=(kc == nk - 1))
                            xr = XR[:, t, j * 512:(j + 1) * 512]
                            if first:
                                P.stt(xr, xr, ALPHA, b[:], ALU.mult, ALU.add)
                            else:
                                P.tt(xr, xr, b[:], ALU.add)
                    first = False
                    ui += nu
                P.barrier()
            with contextlib.ExitStack() as st:
                gb, bb = self.load_ln_params(st, l, 2)
                for t in range(NT):
                    self.layer_norm_tile(st, t, XR[:, t, :], gb[:], bb[:], l, last)
                P.barrier()
        if not last:
            self.halo_exchange(l, "m", 3)

    def halo_exchange(self, l, tag, n):
        P = self.P
        with contextlib.ExitStack() as st:
            hsrc = self.sb(st, f"hsrc{l}{tag}", [128, 2048], F32)
            hall = self.sb(st, f"hall{l}{tag}", [128, 8, 64], F32)
            acc = self.sb(st, f"hacc{l}{tag}", [128, 64], F32)
            P.memset(hsrc[:], 0.0)
            P.copy(hsrc[:, 0:16 * n].rearrange("p (c j) -> p c j", j=n), self.XT[:, :, T - n:T])
            dst = self.exch(f"h{l}{tag}", hsrc[:])
            P.dma(hall[:], dst.rearrange("(r p) x -> p r x", p=128)[:, :, 0:64])
            P.memset(acc[:], 0.0)
            for r in range(NCORES - 1):
                P.stt(acc[:], hall[:, r, :], self.cm[:, r:r + 1], acc[:], ALU.mult, ALU.add)
            P.copy(self.XTh[:, :, 0:n], acc[:, 0:16 * n].rearrange("p (c j) -> p c j", j=n))
            P.barrier()


_CACHE = {}


def make_in_maps(inp, layers, cc=True, wl_dev=None):
    x = inp["x"][0]
    mem = inp["mem"][0]
    memT = np.ascontiguousarray(mem.T.reshape(16, 128, NMEM).transpose(1, 0, 2))
    consts = host_consts()
    Ws = [host_weights(inp, l) for l in layers]
    smalls = [host_small(inp, l) for l in layers]
    pcol = np.stack([s[0] for s in smalls])
    prow = np.concatenate([s[1] for s in smalls], axis=0)
    wgate = np.stack([s[2] for s in smalls])
    wsblk = np.stack([host_sblock(inp, l) for l in layers])
    maps = []
    for c in range(NCORES):
        cm = np.zeros((128, 24), np.float32)
        for r in range(8):
            cm[:, r] = 1.0 if r == c - 1 else 0.0
            cm[:, 8 + r] = 1.0 if r < c else 0.0
            cm[:, 16 + r] = 0.0 if r < c else 1.0
        xh = np.zeros((3, D), np.float32)
        if c > 0:
            xh = x[c * T - 3:c * T]
        xhT = np.ascontiguousarray(xh.T.reshape(16, 128, 3).transpose(1, 0, 2))
        m = {"x": np.ascontiguousarray(x[c * T:(c + 1) * T]), "xhalo": xhT, "memT": memT, "consts": consts,
             "cmask": cm, "pcol": pcol, "prow": prow, "wgate": wgate, "wsblk": wsblk}
        for i in range(len(layers)):
            if cc:
                m[f"wsh{i}"] = np.ascontiguousarray(Ws[i][16 * c:16 * (c + 1)])
            else:
                m[f"wsh{i}"] = np.ascontiguousarray(Ws[i][:, :wl_dev])
        maps.append(m)
    return maps


WL_A = (0, 90112)
WL_B = (0, 253952)
WL_C = (253952, WL)


def _halo(xfull, c, n):
    xh = np.zeros((3, D), np.float32)
    if c > 0:
        xh[0:n] = xfull[c * T - n:c * T]
    return np.ascontiguousarray(xh.T.reshape(16, 128, 3).transpose(1, 0, 2))


def _cmask(c):
    cm = np.zeros((128, 24), np.float32)
    for r in range(8):
        cm[:, r] = 1.0 if r == c - 1 else 0.0
        cm[:, 8 + r] = 1.0 if r < c else 0.0
        cm[:, 16 + r] = 0.0 if r < c else 1.0
    return cm


def _get_prog(kind):
    if kind not in _CACHE:
        lo, hi = {"A": WL_A, "B": WL_B, "C": WL_C}[kind]
        _CACHE[kind] = Builder(1, cc=False, wl_dev=hi - lo, kind=kind, wl_base=lo).build()
    return _CACHE[kind]


def kernel(**inputs):
    inp = {k: np.asarray(v) for k, v in inputs.items()}
    x = np.ascontiguousarray(inp["x"][0])
    mem = inp["mem"][0]
    memT = np.ascontiguousarray(mem.T.reshape(16, 128, NMEM).transpose(1, 0, 2))
    consts = host_consts()
    cms = [_cmask(c) for c in range(NCORES)]
    cores = list(range(NCORES))
    for l in range(DEPTH):
        Wl = host_weights(inp, l)
        pc, pr, wg = host_small(inp, l)
        common = {"memT": memT, "consts": consts, "pcol": pc[None], "prow": pr, "wgate": wg[None],
                  "wsblk": host_sblock(inp, l)[None]}

        def maps(kind, xfull, n, extra=None):
            lo, hi = {"A": WL_A, "B": WL_B, "C": WL_C}[kind]
            wsl = np.ascontiguousarray(Wl[:, lo:hi])
            out = []
            for c in cores:
                m = dict(common)
                m["x"] = np.ascontiguousarray(xfull[c * T:(c + 1) * T])
                m["xT"] = np.ascontiguousarray(m["x"].T.reshape(16, 128, T).transpose(1, 0, 2))
                m["xhalo"] = _halo(xfull, c, n)
                m["cmask"] = cms[c]
                m["wsh0"] = wsl
                if extra:
                    m.update(extra)
                out.append(m)
            return out

        ra = run_bass_kernel_spmd(_get_prog("A"), maps("A", x, 3), core_ids=cores)
        gC = np.concatenate([ra.results[c]["outC"] for c in cores], axis=0)
        gD = np.concatenate([ra.results[c]["outD"] for c in cores], axis=0)
        rb = run_bass_kernel_spmd(_get_prog("B"), maps("B", x, 3, {"gC": gC, "gD": gD}), core_ids=cores)
        x2 = np.concatenate([rb.results[c]["out"] for c in cores], axis=0)
        rc = run_bass_kernel_spmd(_get_prog("C"), maps("C", x2, 2), core_ids=cores)
        x = np.concatenate([rc.results[c]["out"] for c in cores], axis=0)
    return x.reshape(1, SEQ, D).astype(np.float32)
```
